# Optimizing a Trainium2 kernel written in Bass

```python
import math
import jax
import jax.numpy as jnp
from jax import lax
import numpy as np

D_MODEL = 1024
BATCH = 2
SEQ = 16384
DEPTH = 2

F32 = jnp.float32
CTX_LEN = 256
GRID_W = 64
CHUNK = 64
NORM_EPS = 1e-6
HEAD_NORM_EPS = 1e-5

N_RET_HEADS = 4
RET_DK = 64
RET_DV = 128
RET_QK = N_RET_HEADS * RET_DK
RET_V = N_RET_HEADS * RET_DV
RET_LOG_DECAY_FWD = tuple(math.log1p(-2.0 ** (-5.0 - h)) for h in range(N_RET_HEADS))
RET_LOG_DECAY_BWD = tuple(math.log1p(-2.0 ** (-5.5 - h)) for h in range(N_RET_HEADS))

N_GLA_HEADS = 4
GLA_DK = 64
GLA_DV = 128
GLA_QK = N_GLA_HEADS * GLA_DK
GLA_V = N_GLA_HEADS * GLA_DV
GLA_LOWRANK = 16
GLA_GATE_NORM = 16.0

D_HY = 512
HY_BANDS = 16
HY_EMB = 1 + 2 * HY_BANDS
HY_FILTER_WIDTH = 64
HY_INNER = 2
HY_TARGET = 1e-2
HY_FAST_PCT = 0.3
HY_SLOW_PCT = 1.5
HY_MIN_DECAY = math.log(HY_TARGET) / HY_SLOW_PCT
HY_MAX_DECAY = math.log(HY_TARGET) / HY_FAST_PCT
HY_SHIFT = 0.05
HY_FILTER_INIT = 0.05

N_BRANCH = 3
BRANCH_W = 512
D_FF = 2816

IN_SPLITS = (RET_QK, RET_QK, RET_V, RET_V, GLA_QK, GLA_QK, GLA_V, GLA_V, 2 * GLA_LOWRANK, 3 * D_HY, N_BRANCH * D_MODEL)
D_IN = sum(IN_SPLITS)
IN_OFFSETS = tuple(sum(IN_SPLITS[:i + 1]) for i in range(len(IN_SPLITS) - 1))

kernel_name = 'hybrid_ret_gla_hyena_dit'


def rms_norm(x, g):
    xf = x.astype(F32)
    y = xf * lax.rsqrt(jnp.mean(xf * xf, axis=-1, keepdims=True) + NORM_EPS)
    return (y * g.astype(F32)).astype(x.dtype)


def modulate(x, g, shift, scale):
    return rms_norm(x, g) * (1.0 + scale) + shift


def head_norm(o):
    mu = jnp.mean(o, axis=-1, keepdims=True)
    var = jnp.mean(jnp.square(o - mu), axis=-1, keepdims=True)
    return (o - mu) * lax.rsqrt(var + HEAD_NORM_EPS)


def to_heads(z, n):
    b, t, w = z.shape
    return z.reshape(b, t, n, w // n).transpose(0, 2, 1, 3)


def from_heads(z):
    b, n, t, d = z.shape
    return z.transpose(0, 2, 1, 3).reshape(b, t, n * d)


def dwconv1d(z, w, bias):
    L = z.shape[1]
    zp = jnp.pad(z, ((0, 0), (1, 1), (0, 0)))
    return bias + zp[:, :L] * w[0] + zp[:, 1:L + 1] * w[1] + zp[:, 2:] * w[2]


def dwconv2d(z, w, bias):
    R, W = z.shape[1], z.shape[2]
    zp = jnp.pad(z, ((0, 0), (1, 1), (1, 1), (0, 0)))
    out = bias
    for i in range(3):
        for j in range(3):
            out = out + zp[:, i:i + R, j:j + W] * w[i, j]
    return out


def chunked_scan(q, k, v, log_a, h0, strict, need_out=True):
    q, k, v, log_a = (z.astype(F32) for z in (q, k, v, log_a))
    b_, h_, t, _ = q.shape
    dv = v.shape[-1]
    n = t // CHUNK
    split = lambda z: z.reshape(b_, h_, n, CHUNK, z.shape[-1])
    qc, kc, vc, la = split(q), split(k), split(v), split(log_a)
    cum = jnp.cumsum(la, axis=3)
    cum_last = cum[:, :, :, -1:]
    k_out = kc * jnp.exp(cum_last - cum)
    a_chunk = jnp.exp(cum_last[:, :, :, 0])
    lead = lambda z: jnp.moveaxis(z, 2, 0)
    xs = (lead(k_out), lead(vc), lead(a_chunk))
    if need_out:
        q_in = qc * jnp.exp(cum)
        idx = jnp.arange(CHUNK)
        mask = idx[:, None] > idx[None, :] if strict else idx[:, None] >= idx[None, :]
        if la.shape[-1] == 1:
            cs = cum[..., 0]
            diff = jnp.where(mask, cs[..., :, None] - cs[..., None, :], 0.0)
            decay = jnp.where(mask, jnp.exp(diff), 0.0)
            scores = jnp.einsum('bhncd,bhnsd->bhncs', qc, kc) * decay
        else:
            scores = jnp.einsum('bhncd,bhnsd->bhncs', q_in, kc * jnp.exp(-cum))
            scores = jnp.where(mask, scores, 0.0)
        o_intra = jnp.einsum('bhncs,bhnse->bhnce', scores, vc)
        xs = xs + (lead(q_in),)

    def step(h, xs_i):
        ki, vi, ai = xs_i[:3]
        o = jnp.einsum('bhcd,bhde->bhce', xs_i[3], h) if need_out else None
        h_new = ai[..., None] * h + jnp.einsum('bhcd,bhce->bhde', ki, vi)
        return h_new, o

    h_t, o_inter = lax.scan(step, h0, xs)
    if not need_out:
        return None, h_t
    o = o_intra + jnp.moveaxis(o_inter, 0, 2)
    return o.reshape(b_, h_, t, dv), h_t


def bidir_recurrence(ctx_qkv, lat_qkv, ctx_log_a, lat_log_a, need_ctx_out):
    bsz, nh, _, dk = ctx_qkv[0].shape
    dv = ctx_qkv[2].shape[-1]
    o_ctx, o_lat = None, None
    for d in range(2):
        rev = d == 1
        f = (lambda z: jnp.flip(z, axis=2)) if rev else (lambda z: z)
        h0 = jnp.zeros((bsz, nh, dk, dv), F32)
        oc, hc = chunked_scan(*[f(z) for z in ctx_qkv], f(ctx_log_a[d]), h0, rev, need_ctx_out)
        ol, _ = chunked_scan(*[f(z) for z in lat_qkv], f(lat_log_a[d]), hc, rev)
        o_lat = f(ol) if o_lat is None else o_lat + f(ol)
        if need_ctx_out:
            o_ctx = f(oc) if o_ctx is None else o_ctx + f(oc)
    return o_ctx, o_lat


def retention_branch(p_ctx, p_lat, need_ctx_out):
    def prep(q, k, v):
        return (to_heads(q, N_RET_HEADS).astype(F32),
                to_heads(k, N_RET_HEADS).astype(F32) * RET_DK ** -0.5,
                to_heads(v, N_RET_HEADS).astype(F32))

    def log_decay(q):
        b, h, t, _ = q.shape
        return tuple(jnp.broadcast_to(jnp.asarray(g, F32)[None, :, None, None], (b, h, t, 1))
                     for g in (RET_LOG_DECAY_FWD, RET_LOG_DECAY_BWD))

    qkv_c, qkv_l = prep(*p_ctx[:3]), prep(*p_lat[:3])
    o_c, o_l = bidir_recurrence(qkv_c, qkv_l, log_decay(qkv_c[0]), log_decay(qkv_l[0]), need_ctx_out)
    out = lambda o, g: from_heads(head_norm(o)) * jax.nn.silu(g.astype(F32))
    return (out(o_c, p_ctx[3]) if need_ctx_out else None), out(o_l, p_lat[3])


def gla_branch(p_ctx, p_lat, wa2, ba, need_ctx_out):
    def prep(q, k, v, r, lr):
        qkv = (to_heads(q, N_GLA_HEADS).astype(F32) * GLA_DK ** -0.5,
               to_heads(k, N_GLA_HEADS).astype(F32),
               to_heads(v, N_GLA_HEADS).astype(F32))
        lr_dirs = jnp.split(lr.astype(F32), 2, axis=-1)
        log_a = tuple(to_heads(jax.nn.log_sigmoid(lr_dirs[d] @ wa2[d] + ba[d]) / GLA_GATE_NORM, N_GLA_HEADS)
                      for d in range(2))
        return qkv, log_a

    qkv_c, la_c = prep(*p_ctx)
    qkv_l, la_l = prep(*p_lat)
    o_c, o_l = bidir_recurrence(qkv_c, qkv_l, la_c, la_l, need_ctx_out)
    out = lambda o, r: from_heads(head_norm(o)) * jax.nn.silu(r.astype(F32))
    return (out(o_c, p_ctx[3]) if need_ctx_out else None), out(o_l, p_lat[3])


def hyena_filters(length, w1, b1, w2, b2, w3, freq):
    t = jnp.linspace(0.0, 1.0, length, dtype=F32)[:, None]
    ang = (2.0 * math.pi / length) * jnp.arange(length, dtype=F32)[:, None] \
        * jnp.linspace(1e-4, HY_BANDS - 1.0, HY_BANDS, dtype=F32)[None, :]
    z = jnp.concatenate([t, jnp.cos(ang), -jnp.sin(ang)], axis=-1)
    hdn = jnp.sin(freq * (z @ w1 + b1))
    for i in range(HY_INNER):
        hdn = jnp.sin(freq * (hdn @ w2[i] + b2[i]))
    h = (hdn @ w3).reshape(length, 2, D_HY)
    deltas = jnp.abs(jnp.linspace(HY_MIN_DECAY, HY_MAX_DECAY, D_HY, dtype=F32))
    window = jnp.exp(-t * deltas) + HY_SHIFT
    h = h * window[:, None, :]
    return h[:, 0], h[:, 1]


def fft_long_conv(z, h_fwd, h_bwd):
    L = z.shape[1]
    k = jnp.concatenate([h_fwd, jnp.zeros_like(h_fwd[:1]), jnp.flip(h_bwd[1:], axis=0)], axis=0)
    zf = jnp.fft.rfft(z, n=2 * L, axis=1)
    kf = jnp.fft.rfft(k, axis=0)
    return jnp.fft.irfft(zf * kf[None], n=2 * L, axis=1)[:, :L]


def hyena_branch(p, h_fwd, h_bwd, short_w, short_b, bias):
    u = dwconv1d(p, short_w, short_b)
    x0, x1, v = jnp.split(u, 3, axis=-1)
    z = (x1 * v).astype(F32)
    y = fft_long_conv(z, h_fwd, h_bwd) + z * bias
    return x0.astype(F32) * y


def merge_branches(branches, gate_logits, w_branch, w_out):
    gates = jnp.split(jax.nn.sigmoid(gate_logits.astype(F32)), N_BRANCH, axis=-1)
    mixed = gates[0] * (branches[0] @ w_branch[0])
    for g in range(1, N_BRANCH):
        mixed = mixed + gates[g] * (branches[g] @ w_branch[g])
    return mixed @ w_out


def token_mixer(h_ctx, h_lat, lp, need_ctx_out):
    p_ctx = jnp.split(h_ctx @ lp['w_in'], IN_OFFSETS, axis=-1)
    p_lat = jnp.split(h_lat @ lp['w_in'], IN_OFFSETS, axis=-1)
    ret_c, ret_l = retention_branch(p_ctx[0:4], p_lat[0:4], need_ctx_out)
    gla_c, gla_l = gla_branch(p_ctx[4:9], p_lat[4:9], lp['gla_wa2'], lp['gla_ba'], need_ctx_out)
    filt = (lp['hy_w1'], lp['hy_b1'], lp['hy_w2'], lp['hy_b2'], lp['hy_w3'], lp['hy_freq'])
    hy_args = (lp['hy_short_w'], lp['hy_short_b'], lp['hy_bias'])
    hy_l = hyena_branch(p_lat[9], *hyena_filters(h_lat.shape[1], *filt), *hy_args)
    mix_l = merge_branches((ret_l, gla_l, hy_l), p_lat[10], lp['w_branch'], lp['w_out'])
    if not need_ctx_out:
        return None, mix_l
    hy_c = hyena_branch(p_ctx[9], *hyena_filters(h_ctx.shape[1], *filt), *hy_args)
    mix_c = merge_branches((ret_c, gla_c, hy_c), p_ctx[10], lp['w_branch'], lp['w_out'])
    return mix_c, mix_l


def conv_glu(h, rows, cols, lp):
    b, t, _ = h.shape
    a, v = jnp.split(h @ lp['w_up'], 2, axis=-1)
    a = dwconv2d(a.reshape(b, rows, cols, D_FF), lp['ffn_conv_w'], lp['ffn_conv_b']).reshape(b, t, D_FF)
    return (jax.nn.gelu(a, approximate=False) * v) @ lp['w_down']


def setup_inputs(seed: int = 0) -> dict:
    key = jax.random.key(seed)
    ks = iter(jax.random.split(key, 32))
    nrm = lambda shape, s: s * jax.random.normal(next(ks), shape, F32)
    L = DEPTH
    return {
        'x': nrm((BATCH, SEQ, D_MODEL), 1.0),
        'c': nrm((BATCH, D_MODEL), 1.0),
        'ctx': nrm((BATCH, CTX_LEN, D_MODEL), 1.0),
        'c_ctx': nrm((D_MODEL,), 1.0),
        'ada_w': nrm((L, D_MODEL, 6 * D_MODEL), 0.5 * D_MODEL ** -0.5),
        'ada_b': nrm((L, 6 * D_MODEL), 0.02),
        'norm1_g': 1.0 + nrm((L, D_MODEL), 0.02),
        'w_in': nrm((L, D_MODEL, D_IN), D_MODEL ** -0.5),
        'gla_wa2': nrm((L, 2, GLA_LOWRANK, GLA_QK), GLA_LOWRANK ** -0.5),
        'gla_ba': nrm((L, 2, GLA_QK), 0.1),
        'hy_short_w': nrm((L, 3, 3 * D_HY), 3 ** -0.5),
        'hy_short_b': nrm((L, 3 * D_HY), 0.02),
        'hy_w1': nrm((L, HY_EMB, HY_FILTER_WIDTH), HY_EMB ** -0.5),
        'hy_b1': nrm((L, HY_FILTER_WIDTH), 0.1),
        'hy_w2': nrm((L, HY_INNER, HY_FILTER_WIDTH, HY_FILTER_WIDTH), HY_FILTER_WIDTH ** -0.5),
        'hy_b2': nrm((L, HY_INNER, HY_FILTER_WIDTH), 0.1),
        'hy_w3': nrm((L, HY_FILTER_WIDTH, 2 * D_HY), HY_FILTER_INIT * HY_FILTER_WIDTH ** -0.5),
        'hy_freq': 1.0 + nrm((L, HY_FILTER_WIDTH), 0.1),
        'hy_bias': nrm((L, D_HY), 1.0),
        'w_branch': nrm((L, N_BRANCH, BRANCH_W, D_MODEL), BRANCH_W ** -0.5),
        'w_out': nrm((L, D_MODEL, D_MODEL), D_MODEL ** -0.5),
        'norm2_g': 1.0 + nrm((L, D_MODEL), 0.02),
        'w_up': nrm((L, D_MODEL, 2 * D_FF), D_MODEL ** -0.5),
        'ffn_conv_w': nrm((L, 3, 3, D_FF), 1.0 / 3.0),
        'ffn_conv_b': nrm((L, D_FF), 0.02),
        'w_down': nrm((L, D_FF, D_MODEL), D_FF ** -0.5),
        'final_g': 1.0 + nrm((D_MODEL,), 0.02),
    }


def reference(x, c, ctx, c_ctx, ada_w, ada_b, norm1_g, w_in, gla_wa2, gla_ba, hy_short_w, hy_short_b,
              hy_w1, hy_b1, hy_w2, hy_b2, hy_w3, hy_freq, hy_bias, w_branch, w_out, norm2_g, w_up,
              ffn_conv_w, ffn_conv_b, w_down, final_g):
    rows = x.shape[1] // GRID_W
    x_lat, x_ctx = x, ctx
    s_lat = jax.nn.silu(c)
    s_ctx = jax.nn.silu(c_ctx)
    for l in range(DEPTH):
        last = l == DEPTH - 1
        lp = dict(w_in=w_in[l], gla_wa2=gla_wa2[l], gla_ba=gla_ba[l], hy_short_w=hy_short_w[l],
                  hy_short_b=hy_short_b[l], hy_w1=hy_w1[l], hy_b1=hy_b1[l], hy_w2=hy_w2[l], hy_b2=hy_b2[l],
                  hy_w3=hy_w3[l], hy_freq=hy_freq[l], hy_bias=hy_bias[l], w_branch=w_branch[l],
                  w_out=w_out[l], w_up=w_up[l], ffn_conv_w=ffn_conv_w[l], ffn_conv_b=ffn_conv_b[l],
                  w_down=w_down[l])
        sh1, sc1, g1, sh2, sc2, g2 = jnp.split((s_lat @ ada_w[l] + ada_b[l])[:, None, :], 6, axis=-1)
        csh1, csc1, cg1, csh2, csc2, cg2 = jnp.split(s_ctx @ ada_w[l] + ada_b[l], 6, axis=-1)
        h_lat = modulate(x_lat, norm1_g[l], sh1, sc1)
        h_ctx = modulate(x_ctx, norm1_g[l], csh1, csc1)
        mix_c, mix_l = token_mixer(h_ctx, h_lat, lp, not last)
        x_lat = x_lat + g1 * mix_l
        x_lat = x_lat + g2 * conv_glu(modulate(x_lat, norm2_g[l], sh2, sc2), rows, GRID_W, lp)
        if not last:
            x_ctx = x_ctx + cg1 * mix_c
            x_ctx = x_ctx + cg2 * conv_glu(modulate(x_ctx, norm2_g[l], csh2, csc2), 1, x_ctx.shape[1], lp)
    return rms_norm(x_lat, final_g)
```

```python
import contextlib
import math
import os
import numpy as np
import concourse.bass as bass
import concourse.mybir as mybir
from concourse.ap import AP
from concourse.bass_utils import run_bass_kernel_spmd
import ml_dtypes

F32 = mybir.dt.float32
BF16 = mybir.dt.bfloat16
AF = mybir.ActivationFunctionType
ALU = mybir.AluOpType
AX = mybir.AxisListType

ENGS = ("pe", "act", "dve", "pool", "sp")
NDMASEM = 24
DMAQ = ("sp", "act", "pool")


class _Op:
    __slots__ = ("eng", "fn", "deps", "isdma", "needed", "semval", "dsem", "dval", "prevdma")

    def __init__(self, eng, fn, isdma):
        self.eng = eng
        self.fn = fn
        self.deps = []
        self.isdma = isdma
        self.needed = False
        self.semval = 0
        self.dsem = -1
        self.dval = 0
        self.prevdma = None


class Prog:
    def __init__(self):
        self.nc = bass.Bass("TRN2", target_bir_lowering=False)
        self.streams = {e: [] for e in ENGS}
        self.tok = {}
        self.es = contextlib.ExitStack()
        self.ndma = {q: 0 for q in DMAQ}
        self.dma_last = {q: [None] * NDMASEM for q in DMAQ}
        self.dma_cnt = {q: [0] * NDMASEM for q in DMAQ}
        self.out_dmas = []
        self._n = 0

    def dram(self, name, shape, dt, kind):
        return self.nc.dram_tensor(name, list(shape), dt, kind=kind).ap()

    def sb(self, shape, dt, name=None):
        self._n += 1
        return self.es.enter_context(self.nc.sbuf_tensor(name or f"sb{self._n}", list(shape), dt))

    def ps(self, shape, dt=F32, name=None):
        self._n += 1
        return self.es.enter_context(self.nc.psum_tensor(name or f"ps{self._n}", list(shape), dt))

    def _deps(self, op, r, w):
        deps = op.deps
        for t in r:
            st = self.tok.get(t)
            if st is None:
                st = self.tok[t] = [None, []]
            if st[0] is not None:
                deps.append(st[0])
        for t in w:
            st = self.tok.get(t)
            if st is None:
                st = self.tok[t] = [None, []]
            if st[0] is not None:
                deps.append(st[0])
            deps.extend(st[1])
        for t in r:
            self.tok[t][1].append(op)
        for t in w:
            st = self.tok[t]
            st[0] = op
            st[1] = []

    def add(self, eng, fn, r=(), w=()):
        op = _Op(eng, fn, False)
        self._deps(op, r, w)
        self.streams[eng].append(op)
        return op

    def dma(self, q, out, in_, r=(), w=(), is_out=False, **kw):
        op = _Op(q, (lambda e, out=out, in_=in_, kw=kw: e.dma_start(out=out, in_=in_, **kw)), True)
        self._deps(op, r, w)
        s = self.ndma[q] % NDMASEM
        self.ndma[q] += 1
        op.dsem = (q, s)
        self.dma_cnt[q][s] += 1
        op.dval = 16 * self.dma_cnt[q][s]
        op.prevdma = self.dma_last[q][s]
        self.dma_last[q][s] = op
        op.needed = True
        self.streams[q].append(op)
        if is_out:
            self.out_dmas.append(op)
        return op

    def mm(self, out, lhsT, rhs, start=True, stop=True, r=(), w=(), **kw):
        return self.add("pe", lambda e: e.matmul(out, lhsT, rhs, start=start, stop=stop, **kw), r, w)

    def actf(self, out, in_, func, r=(), w=(), eng="act", **kw):
        return self.add(eng, lambda e: e.activation(out, in_, func, **kw), r, w)

    def tt(self, eng, out, a, b, op, r=(), w=()):
        return self.add(eng, lambda e: e.tensor_tensor(out, a, b, op), r, w)

    def ts(self, eng, out, a, s1, s2, op0, op1=None, r=(), w=()):
        if op1 is None:
            return self.add(eng, lambda e: e.tensor_scalar(out, a, s1, s2, op0), r, w)
        return self.add(eng, lambda e: e.tensor_scalar(out, a, s1, s2, op0, op1), r, w)

    def stt(self, eng, out, a, s, b, op0, op1, r=(), w=()):
        return self.add(eng, lambda e: e.scalar_tensor_tensor(out, a, s, b, op0, op1), r, w)

    def cp(self, eng, out, in_, r=(), w=()):
        if eng == "act":
            return self.add(eng, lambda e: e.copy(out, in_), r, w)
        return self.add(eng, lambda e: e.tensor_copy(out, in_), r, w)

    def memset(self, eng, ap, val, w=()):
        return self.add(eng, lambda e: e.memset(ap, val), (), w)

    def finish(self):
        nc = self.nc
        for e in ENGS:
            for op in self.streams[e]:
                for d in op.deps:
                    if not d.isdma:
                        if d.eng == "pe" and op.eng == "pe" and not op.isdma:
                            continue
                        d.needed = True
        for e in ENGS:
            c = 0
            for op in self.streams[e]:
                if not op.isdma and op.needed:
                    c += 1
                    op.semval = c
        csem = {e: self.es.enter_context(nc.semaphore(f"c_{e}")) for e in ("pe", "act", "dve", "pool")}
        dsem = {(q, i): self.es.enter_context(nc.semaphore(f"d_{q}_{i}")) for q in DMAQ for i in range(NDMASEM)}
        streams = self.streams
        out_dmas = self.out_dmas

        def emit(eng_name, e):
            waited = {}

            def wait(key, sem, val):
                if waited.get(key, 0) >= val:
                    return
                waited[key] = val
                e.wait_ge(sem, val)

            def wait_op(d):
                if d.isdma:
                    wait(("d", d.dsem), dsem[d.dsem], d.dval)
                else:
                    wait(("c", d.eng), csem[d.eng], d.semval)

            for op in streams[eng_name]:
                for d in op.deps:
                    if (not d.isdma) and d.eng == "pe" and eng_name == "pe" and not op.isdma:
                        continue
                    wait_op(d)
                if op.isdma and op.prevdma is not None:
                    wait_op(op.prevdma)
                ins = op.fn(e)
                if op.isdma:
                    ins.then_inc(dsem[op.dsem], 16)
                elif op.needed:
                    ins.then_inc(csem[eng_name], 1)
            if eng_name == "sp":
                for d in out_dmas:
                    wait_op(d)

        with nc.Block() as block:
            @block.sync
            def _(e):
                emit("sp", e)

            @block.tensor
            def _(e):
                emit("pe", e)

            @block.scalar
            def _(e):
                emit("act", e)

            @block.vector
            def _(e):
                emit("dve", e)

            @block.gpsimd
            def _(e):
                emit("pool", e)
        self.es.close()
        return nc


NWP = 37440
NPIECE0 = 10
PW0 = NWP // NPIECE0

def build_k0():
    p = Prog()
    wsl = p.dram("wsl", [128, NWP], F32, "ExternalInput")
    adaw = p.dram("adaw", [2, 1024, 768], F32, "ExternalInput")
    cT = p.dram("cT", [128, 8, 3], F32, "ExternalInput")
    adab = p.dram("adab", [128, 2, 6], F32, "ExternalInput")
    wbf = p.dram("wbf", [128, NWP], BF16, "ExternalOutput")
    modT = p.dram("modT", [128, 2, 6, 3], F32, "ExternalOutput")

    c_sb = p.sb([128, 8, 3], F32)
    s_sb = p.sb([128, 8, 3], F32)
    b_sb = p.sb([128, 2, 6], F32)
    aw = p.sb([128, 2, 8, 768], F32)
    m_sb = p.sb([128, 2, 6, 3], F32)
    pm = p.ps([128, 2, 6, 4], F32)
    p.dma("sp", c_sb[:], cT, w=["c"])
    p.dma("sp", b_sb[:], adab, w=["b"])
    for l in range(2):
        for k in range(8):
            p.dma("sp", aw[:, l, k, :], adaw[l, k * 128:(k + 1) * 128, :], w=[("aw", l, k)])
    p.actf(s_sb[:], c_sb[:], AF.Silu, r=["c"], w=["s"])
    for l in range(2):
        for j in range(6):
            for k in range(8):
                p.mm(pm[:, l, j, 0:3], aw[:, l, k, j * 128:(j + 1) * 128], s_sb[:, k, :],
                     start=(k == 0), stop=(k == 7), r=[("aw", l, k), "s"], w=[("pm", l, j)])
            p.ts("dve", m_sb[:, l, j, :], pm[:, l, j, 0:3], b_sb[:, l, j:j + 1], None, ALU.add,
                 r=[("pm", l, j), "b"], w=["m"])
    p.dma("pool", modT, m_sb[:], r=["m"], is_out=True)

    NB = 3
    fin = [p.sb([128, PW0], F32) for _ in range(NB)]
    fout = [p.sb([128, PW0], BF16) for _ in range(NB)]
    engs = ["dve", "pool", "act"]
    for i in range(NPIECE0):
        s = i % NB
        p.dma("sp", fin[s][:], wsl[:, i * PW0:(i + 1) * PW0], w=[("fin", s)])
        p.cp(engs[i % 3], fout[s][:], fin[s][:], r=[("fin", s)], w=[("fout", s)])
        p.dma("pool", wbf[:, i * PW0:(i + 1) * PW0], fout[s][:], r=[("fout", s)], is_out=True)
    return p.finish()


TE = 4608
NT = 9
EPS1 = 1e-6

def fm_chunks():
    L = []
    for j in range(4): L.append((128 * j, 128, "qk", j))
    for j in range(4): L.append((1536 + 128 * j, 128, "qk", 4 + j))
    for j in range(4): L.append((1024 + 128 * j, 128, "sg", j))
    for j in range(4): L.append((2560 + 128 * j, 128, "sg", 4 + j))
    L.append((3072, 32, "lr", 0))
    for j in range(4): L.append((3104 + 128 * j, 128, "hy", j))
    for j in range(4):
        L.append((3104 + 128 * (4 + j), 128, "hy", 4 + j))
        L.append((3104 + 128 * (8 + j), 128, "hy", 8 + j))
    for j in range(24): L.append((4640 + 128 * j, 128, "gate", j))
    return L


def build_k1():
    p = Prog()
    xT = p.dram("xT", [128, 8, TE], F32, "ExternalInput")
    win = p.dram("win", [1024, 7712], BF16, "ExternalInput")
    modsel = p.dram("modsel", [128, 48, 2], F32, "ExternalInput")
    n1g = p.dram("n1g", [128, 8], F32, "ExternalInput")
    hsw = p.dram("hsw", [128, 12, 4], F32, "ExternalInput")
    vmlr = p.dram("vmlr", [128, 2], F32, "ExternalInput")
    o_qk = p.dram("qk", [8, 128, TE], BF16, "ExternalOutput")
    o_sg = p.dram("sg", [8, 128, TE], BF16, "ExternalOutput")
    o_lr = p.dram("lrT", [32, TE], F32, "ExternalOutput")
    o_gate = p.dram("gate", [24, 128, TE], BF16, "ExternalOutput")
    o_x0 = p.dram("x0", [4, 128, TE], BF16, "ExternalOutput")
    o_z = p.dram("z", [4, 128, TE], BF16, "ExternalOutput")
    o_kv = p.dram("kv", [TE, 1536], BF16, "ExternalOutput")
    winr = win.rearrange("(k p) c -> p k c", p=128)

    hT = p.sb([128, 8, TE], BF16, "hT")
    xs = [p.sb([128, 8, 256], F32) for _ in range(2)]
    sq = p.sb([128, 8, 256], BF16)
    rstd = p.sb([128, 256], F32)
    ones = p.sb([128, 128], BF16)
    ms = p.sb([128, 48, 2], F32)
    gsb = p.sb([128, 8], F32)
    A1 = p.sb([128, 8, 2], F32)
    hs = p.sb([128, 12, 4], F32)
    vm = p.sb([128, 2], F32)
    wr = [p.sb([128, 8, 512], BF16) for _ in range(2)]
    stg = [p.sb([128, 512], BF16) for _ in range(4)]
    stg32 = [p.sb([32, 512], F32) for _ in range(2)]
    PB_ = [p.sb([128, 4612], F32) for _ in range(2)]
    cu = [p.sb([128, 512], F32) for _ in range(4)]
    zst = [p.sb([128, 512], BF16) for _ in range(2)]
    kvst = [p.sb([128, 768], BF16) for _ in range(2)]
    psA = [p.ps([128, 512], F32) for _ in range(4)]
    psB = [p.ps([128, 512], F32) for _ in range(2)]
    psS = p.ps([128, 256], F32)

    p.memset("pool", ones[:], 1.0, w=["ones"])
    epsb = p.sb([128, 1], F32)
    p.memset("pool", epsb[:], EPS1, w=["epsb"])
    p.dma("sp", ms[:], modsel, w=["ms"])
    p.dma("sp", gsb[:], n1g, w=["g"])
    p.dma("sp", hs[:], hsw, w=["hs"])
    p.dma("sp", vm[:], vmlr, w=["vm"])
    for i in range(2):
        p.memset("pool", PB_[i][:], 0.0, w=[("P", i)])
    p.ts("dve", A1[:], ms[:, 8:16, :], 1.0, None, ALU.add, r=["ms"], w=["A1"])
    for r_ in range(2):
        p.tt("dve", A1[:, :, r_], A1[:, :, r_], gsb[:], ALU.mult, r=["A1", "g"], w=["A1"])

    for hf in range(18):
        s = hf % 2
        c0 = hf * 256
        r_ = 1 if hf == 17 else 0
        p.dma("sp", xs[s][:], xT[:, :, c0:c0 + 256], w=[("x", s)])
        p.actf(sq[:], xs[s][:], AF.Square, r=[("x", s)], w=["sq"])
        for k in range(8):
            p.mm(psS[:], ones[:], sq[:, k, :], start=(k == 0), stop=(k == 7), r=["ones", "sq"], w=["psS"])
        p.actf(rstd[:], psS[:], AF.Sqrt, r=["psS"], w=["rstd"], scale=1.0 / 1024.0, bias=epsb[:])
        p.add("dve", lambda e: e.reciprocal(rstd[:], rstd[:]), r=["rstd"], w=["rstd"])
        p.tt("dve", xs[s][:], xs[s][:], rstd[:].unsqueeze(1).broadcast_to([128, 8, 256]), ALU.mult,
             r=[("x", s), "rstd"], w=[("x", s)])
        for k in range(8):
            eng = "act" if k % 2 == 0 else "dve"
            if eng == "act":
                p.actf(hT[:, k, c0:c0 + 256], xs[s][:, k, :], AF.Identity, r=[("x", s), "A1", "ms"],
                       w=[("hT", hf)], scale=A1[:, k, r_:r_ + 1], bias=ms[:, k, r_:r_ + 1])
            else:
                p.ts("dve", hT[:, k, c0:c0 + 256], xs[s][:, k, :], A1[:, k, r_:r_ + 1], ms[:, k, r_:r_ + 1],
                     ALU.mult, ALU.add, r=[("x", s), "A1", "ms"], w=[("hT", hf)])
        if hf == 0:
            p.ts("dve", hT[:, :, 0:128], hT[:, :, 0:128], vm[:, 0:1], None, ALU.mult, r=["vm", ("hT", 0)], w=[("hT", 0)])
        if hf == 16:
            p.ts("dve", hT[:, :, 4224:4352], hT[:, :, 4224:4352], vm[:, 1:2], None, ALU.mult,
                 r=["vm", ("hT", 16)], w=[("hT", 16)])

    chunks = fm_chunks()
    groups = []
    for ch in chunks:
        if groups and groups[-1][0] + groups[-1][1] == ch[0] and groups[-1][1] + ch[1] <= 512 and ch[2] != "hy" and groups[-1][2][0][2] != "hy":
            groups[-1][1] += ch[1]
            groups[-1][2].append(ch)
        else:
            groups.append([ch[0], ch[1], [ch]])
    nst = 0
    npa = 0
    ncu = 0
    nz = 0
    evq = 0
    for gi, (g0, gw, chs) in enumerate(groups):
        ws = gi % 2
        p.dma("sp", wr[ws][:, :, 0:gw], winr[:, :, g0:g0 + gw], w=[("w", ws)])
        for (c0, M, kind, idx) in chs:
            off = c0 - g0
            for t in range(NT):
                ps = psA[npa % 4]; pst = ("psA", npa % 4); npa += 1
                for k in range(8):
                    p.mm(ps[0:M, :], wr[ws][:, k, off:off + M], hT[:, k, t * 512:(t + 1) * 512],
                         start=(k == 0), stop=(k == 7), r=[("w", ws), ("hT", 2 * t), ("hT", 2 * t + 1)], w=[pst])
                cols = slice(t * 512, (t + 1) * 512)
                if kind in ("qk", "sg", "gate"):
                    st = stg[nst % 4]; stt_ = ("stg", nst % 4); nst += 1
                    if kind == "qk":
                        eng = "dve" if evq % 2 == 0 else "act"; evq += 1
                        p.cp(eng, st[:], ps[:], r=[pst], w=[stt_])
                        dst = o_qk[idx, :, cols]
                    elif kind == "sg":
                        p.actf(st[:], ps[:], AF.Silu, r=[pst], w=[stt_])
                        dst = o_sg[idx, :, cols]
                    else:
                        p.actf(st[:], ps[:], AF.Sigmoid, r=[pst], w=[stt_])
                        dst = o_gate[idx, :, cols]
                    p.dma("pool", dst, st[:], r=[stt_], is_out=True)
                elif kind == "lr":
                    st = stg32[t % 2]; stt_ = ("stg32", t % 2)
                    p.cp("dve", st[:], ps[0:32, :], r=[pst], w=[stt_])
                    p.dma("pool", o_lr[:, cols], st[:], r=[stt_], is_out=True)
                else:
                    bi = 0 if idx < 8 else 1
                    buf = PB_[bi]
                    if t < 8:
                        p.cp("dve", buf[:, 1 + t * 512: 1 + (t + 1) * 512], ps[:], r=[pst], w=[("P", bi)])
                    else:
                        p.cp("dve", buf[:, 4097:4353], ps[:, 0:256], r=[pst], w=[("P", bi)])
                        p.cp("dve", buf[:, 4355:4611], ps[:, 256:512], r=[pst], w=[("P", bi)])
            if kind == "hy" and (idx < 4 or idx >= 8):
                pieces = [(1 + 512 * i, 512, 512 * i) for i in range(8)] + [(4097, 256, 4096), (4355, 256, 4352)]
                for (b0, n, tc) in pieces:
                    def conv(bi, j, eng):
                        nonlocal ncu
                        u = cu[ncu % 4]; ut = ("cu", ncu % 4); ncu += 1
                        buf = PB_[bi]
                        p.ts(eng, u[:, 0:n], buf[:, b0:b0 + n], hs[:, j, 1:2], hs[:, j, 3:4], ALU.mult, ALU.add,
                             r=[("P", bi), "hs"], w=[ut])
                        p.stt(eng, u[:, 0:n], buf[:, b0 - 1:b0 - 1 + n], hs[:, j, 0:1], u[:, 0:n], ALU.mult, ALU.add,
                              r=[("P", bi), "hs", ut], w=[ut])
                        p.stt(eng, u[:, 0:n], buf[:, b0 + 1:b0 + 1 + n], hs[:, j, 2:3], u[:, 0:n], ALU.mult, ALU.add,
                              r=[("P", bi), "hs", ut], w=[ut])
                        return u, ut
                    zs = zst[nz % 2]; zt = ("zst", nz % 2); nz += 1
                    if idx < 4:
                        u, ut = conv(0, idx, "dve")
                        p.cp("act", zs[:, 0:n], u[:, 0:n], r=[ut], w=[zt])
                        p.dma("pool", o_x0[idx, :, tc:tc + n], zs[:, 0:n], r=[zt], is_out=True)
                    else:
                        ua, uat = conv(0, idx - 4, "dve")
                        ub, ubt = conv(1, idx, "dve")
                        p.tt("dve", zs[:, 0:n], ua[:, 0:n], ub[:, 0:n], ALU.mult, r=[uat, ubt], w=[zt])
                        p.dma("pool", o_z[idx - 8, :, tc:tc + n], zs[:, 0:n], r=[zt], is_out=True)

    wkv = [PB_[i][:, 0:3072].bitcast(BF16).rearrange("p (k c) -> p k c", k=8) for i in range(2)]
    for gi, c0 in enumerate((256, 1792)):
        p.dma("sp", wkv[gi], winr[:, :, c0:c0 + 768], w=[("P", gi)])
    nkv = 0
    for tb in range(TE // 128):
        for gi in range(2):
            pa = psB[0]; pb = psB[1]
            for k in range(8):
                p.mm(pa[:], hT[:, k, tb * 128:(tb + 1) * 128], wkv[gi][:, k, 0:512], start=(k == 0), stop=(k == 7),
                     r=[("P", gi), ("hT", tb // 2)], w=["psB0"])
            for k in range(8):
                p.mm(pb[:, 0:256], hT[:, k, tb * 128:(tb + 1) * 128], wkv[gi][:, k, 512:768], start=(k == 0), stop=(k == 7),
                     r=[("P", gi), ("hT", tb // 2)], w=["psB1"])
            st = kvst[nkv % 2]; stt_ = ("kvst", nkv % 2); nkv += 1
            p.cp("dve", st[:, 0:512], pa[:], r=["psB0"], w=[stt_])
            p.cp("act", st[:, 512:768], pb[:, 0:256], r=["psB1"], w=[stt_])
            p.dma("pool", o_kv[tb * 128:(tb + 1) * 128, gi * 768:(gi + 1) * 768], st[:], r=[stt_], is_out=True)
    return p.finish()


I32 = mybir.dt.int32
NG = 32
TT = 256 + 512 * NG
NCH = 2 + 4 * NG
HEADS = [0, 1]
STAGE = 9
LL = 16384
HEPS = 1e-5
NLAG = 17
NPIECE = 15
TWO_PI = 2.0 * math.pi


def groups_fwd():
    return [(0, 2)] + [(2 + 4 * i, 4) for i in range(NG)]


def build_k2(do_scan=True, do_hy=True, do_ctx_out=True):
    p = Prog()
    qT = p.dram("qT", [2, 64, TT], BF16, "ExternalInput")
    kT = p.dram("kT", [2, 64, TT], BF16, "ExternalInput")
    ktok = p.dram("ktok", [2, TT, 64], BF16, "ExternalInput")
    vtok = p.dram("vtok", [2, TT, 128], BF16, "ExternalInput")
    lrT = p.dram("lrT", [34, TT], F32, "ExternalInput")
    wa = p.dram("wa", [34, 128], F32, "ExternalInput")
    laR = p.dram("laR", [128, 2, 64], F32, "ExternalInput")
    tri = p.dram("tri", [128, 4, 128], F32, "ExternalInput")
    msk = p.dram("msk", [128, 2, 128], F32, "ExternalInput")
    o_on = p.dram("onT", [2, 128, TT], BF16, "ExternalOutput")
    zf = p.dram("zf", [2, 33, LL], F32, "ExternalInput")
    zfc = p.dram("zfc", [2, 33, 256], F32, "ExternalInput")
    win = p.dram("win", [2, 64, LL], F32, "ExternalInput")
    winc = p.dram("winc", [2, 64, 256], F32, "ExternalInput")
    hw1 = p.dram("hw1", [33, 64], F32, "ExternalInput")
    hb = p.dram("hb", [64, 4], F32, "ExternalInput")
    hw2 = p.dram("hw2", [64, 2, 64], F32, "ExternalInput")
    hw3 = p.dram("hw3", [64, 2, 64], F32, "ExternalInput")
    zrev = p.dram("zrev", [128, 64, 256], BF16, "ExternalInput")
    znat = p.dram("znat", [128, 64, 256], BF16, "ExternalInput")
    zrevc = p.dram("zrevc", [128, 64, 4], BF16, "ExternalInput")
    znatc = p.dram("znatc", [128, 64, 4], BF16, "ExternalInput")
    hbias = p.dram("hbias", [128, 64], F32, "ExternalInput")
    o_y = p.dram("Y", [64, 128, 256], BF16, "ExternalOutput")
    o_yc = p.dram("Yc", [64, 128, 4], BF16, "ExternalOutput")
    KF = p.nc.dram_tensor("KF", [64, 32768], BF16, kind="Internal")
    KFc = p.nc.dram_tensor("KFc", [64, 512], BF16, kind="Internal")

    B = [p.ps([128, 512], F32, f"B{i}") for i in range(8)]

    def bt(i, *h):
        return [("B", i)]

    tri_sb = p.sb([128, 4, 128], F32)
    msk_sb = p.sb([128, 2, 128], F32)
    p.dma("sp", tri_sb[:], tri, w=["tri"])
    p.dma("sp", msk_sb[:], msk, w=["msk"])
    negc = p.sb([128, 1], F32)
    p.memset("pool", negc[:], -1.0 / 16.0, w=["negc"])
    one_c = p.sb([128, 1], F32)
    p.memset("pool", one_c[:], 1.0, w=["one_c"])
    heps_c = p.sb([128, 1], F32)
    p.memset("pool", heps_c[:], HEPS, w=["heps"])
    onesN = p.sb([128, 128], F32)
    p.memset("pool", onesN[:], 1.0 / 128.0, w=["onesN"])

    if do_scan:
        wa_sb = p.sb([34, 128], F32)
        p.dma("sp", wa_sb[:], wa, w=["wa"])
        laR_sb = p.sb([128, 1, 2, 64], F32)
        p.dma("sp", laR_sb[:, 0, :, :], laR, w=["laR"])
        Gst = p.sb([64, NCH, 128], BF16)
        lr_sb = [p.sb([34, 512], F32) for _ in range(2)]
        la_sb = p.sb([128, 4, 2, 64], F32)
        tex = p.sb([128, 4, 2, 64], F32)
        edte = p.sb([128, 4, 2, 64], F32)
        kdec = p.sb([128, 4, 2, 64], BF16)
        A_sb = p.sb([64, 4, 2], F32)
        q_sb = [p.sb([64, 512], BF16) for _ in range(2)]
        k_sb = [p.sb([64, 512], BF16) for _ in range(2)]
        kt_sb = [p.sb([128, 4, 64], BF16) for _ in range(2)]
        vt_sb = [p.sb([128, 4, 128], BF16) for _ in range(2)]
        E_sb = p.sb([64, 2, 2, 512], F32)
        qx = p.sb([64, 2, 512], BF16)
        kx = p.sb([64, 2, 512], BF16)
        tS = p.sb([128, 512], F32)
        tS2 = p.sb([128, 512], F32)
        P_sb = p.sb([128, 512], BF16)
        o32 = p.sb([128, 512], F32)
        dsb = p.sb([128, 512], F32)
        sqs = p.sb([128, 512], F32)
        rs = p.sb([128, 512], F32)
        on_sb = [p.sb([128, 512], BF16) for _ in range(2)]
        Hst = p.sb([64, 4, 128], BF16)
        Hs = p.sb([64, 128], F32)
        Gs = p.sb([64, 128], F32)
        cnt = {"ld": 0, "on": 0, "U": 0}

        def decay(g, c0, n, dirs, need_cum, slot):
            W = n * 128
            if g == 1:
                p.dma("sp", lr_sb[slot][:, 0:W], lrT[:, c0 * 128:c0 * 128 + W], w=[("lr", slot)])
                for c in range(n):
                    p.mm(B[0][:, c * 128:(c + 1) * 128], lr_sb[slot][:, c * 128:(c + 1) * 128], wa_sb[:], r=[("lr", slot), "wa"], w=bt(0))
                psx = B[0][:, 0:n * 128].rearrange("p (c d e) -> p c d e", c=n, d=2)
                for d in dirs:
                    p.actf(tex[:, 0:n, d, :], psx[:, :, d, :], AF.Exp, r=bt(0), w=["tex"], scale=-1.0)
                    p.actf(la_sb[:, 0:n, d, :], tex[:, 0:n, d, :], AF.Ln, r=["tex", "one_c"], w=["la"], bias=one_c[:])
                la = la_sb
                lat = "la"
                ncmp = n
            else:
                la = laR_sb
                lat = "laR"
                ncmp = 1
                if ("ret_done", tuple(dirs), need_cum) in cnt:
                    pass
            key = ("retc", need_cum)
            if g == 0 and key in cnt:
                return cnt[key](n)
            for c in range(ncmp):
                for d in dirs:
                    p.mm(B[0][:, (c * 2 + d) * 64:(c * 2 + d + 1) * 64], tri_sb[:, 2 + d, :], la[:, c, d, :],
                         r=["tri", lat], w=bt(0))
                    if not need_cum:
                        p.mm(B[0][0:64, (c * 2) * 64:(c * 2) * 64 + 1], la[:, c, d, :], negc[:], r=[lat, "negc"], w=bt(0))
            psd = B[0][:, 0:ncmp * 128].rearrange("p (c d e) -> p c d e", c=ncmp, d=2)
            if g == 0:
                ed = p.sb([128, 1, 2, 64], F32)
                As = p.sb([64, 1, 2], F32)
                edt, Ast = ("edR", need_cum), ("AR", need_cum)
            else:
                ed, As, edt, Ast = edte, A_sb, "edte", "A"
            for d in dirs:
                p.actf(ed[:, 0:ncmp, d, :], psd[:, :, d, :], AF.Exp, r=bt(0), w=[edt])
                if not need_cum:
                    p.actf(As[:, 0:ncmp, d], psd[0:64, :, 0, 0], AF.Exp, r=bt(0), w=[Ast])
            res = {"ed": ed, "edt": edt, "A": As, "At": Ast}
            if need_cum:
                if g == 0:
                    Et_ = p.sb([64, 2, 2, 128], F32)
                    Ett = "ER"
                else:
                    Et_, Ett = E_sb, "E"
                for d in (0, 1):
                    for c in range(ncmp):
                        p.mm(B[2 + d][0:64, c * 128:(c + 1) * 128], la[:, c, d, :], tri_sb[:, d, :], r=[lat, "tri"], w=bt(2 + d))
                    p.actf(Et_[:, d, 0, 0:ncmp * 128], B[2 + d][0:64, 0:ncmp * 128], AF.Exp, r=bt(2 + d), w=[Ett])
                    p.actf(Et_[:, d, 1, 0:ncmp * 128], B[2 + d][0:64, 0:ncmp * 128], AF.Exp, r=bt(2 + d), w=[Ett], scale=-1.0)
                res["E"] = Et_
                res["Et"] = Ett
            if g == 0:
                def mk(nn, res=res):
                    out = {"edt": res["edt"], "At": (res["Et"] if "E" in res else res["At"])}
                    out["ed"] = lambda d: res["ed"][:, 0:1, d, :].broadcast_to([128, nn, 64])
                    if "E" in res:
                        out["A"] = lambda c, d: res["E"][:, d, 0, (127 if d == 0 else 0):(128 if d == 0 else 1)]
                    else:
                        out["A"] = lambda c, d: res["A"][:, 0, d:d + 1]
                    if "E" in res:
                        out["Et"] = res["Et"]
                        out["E"] = lambda d, i: res["E"][:, d, i, :].unsqueeze(1).broadcast_to([64, nn, 128])
                    return out
                cnt[key] = mk
                return mk(n)
            out = {"edt": edt, "At": Ast}
            out["ed"] = lambda d: ed[:, 0:n, d, :]
            out["A"] = lambda c, d: As[:, c, d:d + 1]
            if need_cum:
                out["A"] = lambda c, d: E_sb[:, d, 0, c * 128 + (127 if d == 0 else 0):c * 128 + (128 if d == 0 else 1)]
                out["At"] = res["Et"]
                out["Et"] = res["Et"]
                out["E"] = lambda d, i: E_sb[:, d, i, 0:n * 128].rearrange("p (c t) -> p c t", c=n)
            return out

        def load_kv(g, c0, n, slot, with_qk):
            W = n * 128
            p.dma("sp", kt_sb[slot][:, 0:n, :], ktok[g, c0 * 128:c0 * 128 + W, :].rearrange("(c p) d -> p c d", p=128), w=[("kt", slot)])
            p.dma("sp", vt_sb[slot][:, 0:n, :], vtok[g, c0 * 128:c0 * 128 + W, :].rearrange("(c p) d -> p c d", p=128), w=[("vt", slot)])
            if with_qk:
                p.dma("sp", q_sb[slot][:, 0:W], qT[g, :, c0 * 128:c0 * 128 + W], w=[("q", slot)])
                p.dma("sp", k_sb[slot][:, 0:W], kT[g, :, c0 * 128:c0 * 128 + W], w=[("k", slot)])

        for g in HEADS:
            p.memset("pool", Gs[:], 0.0, w=["G"])
            grpsA = [(0, 2)] + [(2 + 4 * i, 4) for i in reversed(range(NG))]
            for (c0, n) in grpsA:
                slot = cnt["ld"] % 2
                cnt["ld"] += 1
                load_kv(g, c0, n, slot, False)
                dk = decay(g, c0, n, (1,), False, slot)
                p.tt("dve", kdec[:, 0:n, 1, :], kt_sb[slot][:, 0:n, :], dk["ed"](1), ALU.mult, r=[("kt", slot), dk["edt"]], w=["kdec"])
                for c in reversed(range(n)):
                    j = c0 + c
                    p.cp("pool", Gst[:, j, :], Gs[:], r=["G"], w=[("Gst", j)])
                    us = cnt["U"] % 2
                    cnt["U"] += 1
                    ub = (1, 7)[us]
                    p.mm(B[ub][0:64, 0:128], kdec[:, c, 1, :], vt_sb[slot][:, c, :], r=["kdec", ("vt", slot)], w=bt(ub))
                    p.stt("dve", Gs[:], Gs[:], dk["A"](c, 1), B[ub][0:64, 0:128], ALU.mult, ALU.add,
                          r=["G", dk["At"]] + bt(ub), w=["G"])
            p.memset("pool", Hs[:], 0.0, w=["H"])
            for (c0, n) in (groups_fwd() if STAGE >= 2 else []):
                W = n * 128
                slot = cnt["ld"] % 2
                cnt["ld"] += 1
                load_kv(g, c0, n, slot, True)
                dk = decay(g, c0, n, (0, 1), True, slot)
                q3 = q_sb[slot][:, 0:W].rearrange("p (c t) -> p c t", c=n)
                k3 = k_sb[slot][:, 0:W].rearrange("p (c t) -> p c t", c=n)
                for d in (0, 1):
                    p.stt("dve", qx[:, d, 0:W].rearrange("p (c t) -> p c t", c=n), q3, 0.125, dk["E"](d, 0), ALU.mult, ALU.mult,
                          r=[("q", slot), dk["Et"]], w=["qx"])
                    p.tt("dve", kx[:, d, 0:W].rearrange("p (c t) -> p c t", c=n), k3, dk["E"](d, 1), ALU.mult,
                         r=[("k", slot), dk["Et"]], w=["kx"])
                p.tt("dve", kdec[:, 0:n, 0, :], kt_sb[slot][:, 0:n, :], dk["ed"](0), ALU.mult, r=[("kt", slot), dk["edt"]], w=["kdec"])
                for d in (0, 1):
                    for c in range(n):
                        p.mm(B[4 + d][:, c * 128:(c + 1) * 128], kx[:, d, c * 128:(c + 1) * 128], qx[:, d, c * 128:(c + 1) * 128],
                             r=["kx", "qx"], w=bt(4 + d))
                m0 = msk_sb[:, 0, :].unsqueeze(1).broadcast_to([128, n, 128])
                m1 = msk_sb[:, 1, :].unsqueeze(1).broadcast_to([128, n, 128])
                v3 = lambda t: t[:, 0:W].rearrange("p (c t) -> p c t", c=n)
                p.tt("dve", v3(tS), v3(B[4]), m0, ALU.mult, r=bt(4) + ["msk"], w=["tS"])
                p.tt("dve", v3(tS2), v3(B[5]), m1, ALU.mult, r=bt(5) + ["msk"], w=["tS2"])
                p.tt("pool", P_sb[:, 0:W], tS[:, 0:W], tS2[:, 0:W], ALU.add, r=["tS", "tS2"], w=["P"])
                for c in range(n):
                    j = c0 + c
                    p.cp("pool", Hst[:, c, :], Hs[:], r=["H"], w=["Hst"])
                    us = cnt["U"] % 2
                    cnt["U"] += 1
                    ub = (1, 7)[us]
                    p.mm(B[ub][0:64, 0:128], kdec[:, c, 0, :], vt_sb[slot][:, c, :], r=["kdec", ("vt", slot)], w=bt(ub))
                    p.stt("dve", Hs[:], Hs[:], dk["A"](c, 0), B[ub][0:64, 0:128], ALU.mult, ALU.add,
                          r=["H", dk["At"]] + bt(ub), w=["H"])
                    oc = B[6][:, c * 128:(c + 1) * 128]
                    p.mm(oc, vt_sb[slot][:, c, :], P_sb[:, c * 128:(c + 1) * 128], start=True, stop=False, r=[("vt", slot), "P"], w=bt(6))
                    p.mm(oc, Hst[:, c, :], qx[:, 0, c * 128:(c + 1) * 128], start=False, stop=False, r=["Hst", "qx"], w=bt(6))
                    p.mm(oc, Gst[:, j, :], qx[:, 1, c * 128:(c + 1) * 128], start=False, stop=True, r=[("Gst", j), "qx"], w=bt(6))
                p.cp("act", o32[:, 0:W], B[6][:, 0:W], r=bt(6), w=["o32"])
                p.mm(B[2][:, 0:W], onesN[:], o32[:, 0:W], r=["onesN", "o32"], w=bt(2))
                p.tt("dve", dsb[:, 0:W], o32[:, 0:W], B[2][:, 0:W], ALU.subtract, r=["o32"] + bt(2), w=["d"])
                p.actf(sqs[:, 0:W], dsb[:, 0:W], AF.Square, r=["d"], w=["sq"])
                p.mm(B[3][:, 0:W], onesN[:], sqs[:, 0:W], r=["onesN", "sq"], w=bt(3))
                p.actf(rs[:, 0:W], B[3][:, 0:W], AF.Sqrt, r=bt(3) + ["heps"], w=["rs"], bias=heps_c[:])
                p.add("dve", lambda e, W=W: e.reciprocal(rs[:, 0:W], rs[:, 0:W]), r=["rs"], w=["rs"])
                os_ = cnt["on"] % 2
                cnt["on"] += 1
                p.tt("dve", on_sb[os_][:, 0:W], dsb[:, 0:W], rs[:, 0:W], ALU.mult, r=["d", "rs"], w=[("on", os_)])
                p.dma("pool", o_on[g, :, c0 * 128:c0 * 128 + W], on_sb[os_][:, 0:W], r=[("on", os_)], is_out=True)

    if do_hy:
        w1_sb = p.sb([33, 64], F32)
        hb_sb = p.sb([64, 4], F32)
        w2_sb = p.sb([64, 2, 64], F32)
        w3_sb = p.sb([64, 2, 64], F32)
        fb_sb = p.sb([64, 3], F32)
        p.dma("sp", w1_sb[:], hw1, w=["w1"])
        p.dma("sp", hb_sb[:], hb, w=["hb"])
        p.dma("sp", w2_sb[:], hw2, w=["w2"])
        p.dma("sp", w3_sb[:], hw3, w=["w3"])
        p.ts("dve", fb_sb[:], hb_sb[:, 0:3], hb_sb[:, 3:4], None, ALU.mult, r=["hb"], w=["fb"])
        zt_sb = [p.sb([33, 512], F32) for _ in range(2)]
        wt_sb = [p.sb([64, 512], F32) for _ in range(2)]
        a_sb = p.sb([64, 512], F32)
        ki_sb = p.sb([64, 512], I32)
        r_sb = p.sb([64, 512], F32)
        h_sb = [p.sb([64, 512], F32) for _ in range(2)]
        fo_sb = [p.sb([64, 512], BF16) for _ in range(2)]
        mc = {"t": 0}
        kf_toks = {}

        def mlp(zsrc, wsrc, L, kf_t, kf_len):
            ntile = max(1, L // 512)
            n = min(L, 512)
            for run_ in (1, 0):
                for t in range(ntile):
                    s = mc["t"] % 2
                    mc["t"] += 1
                    p.dma("sp", zt_sb[s][:, 0:n], zsrc[run_, :, t * n:(t + 1) * n], w=[("zt", s)])
                    p.dma("sp", wt_sb[s][:, 0:n], wsrc[run_, :, t * n:(t + 1) * n], w=[("wt", s)])
                    bk = B[s]
                    cur_l, cur_r, cur_t = w1_sb[:], zt_sb[s][:, 0:n], ["w1", ("zt", s)]
                    for layer in range(3):
                        p.mm(bk[0:64, 0:n], cur_l, cur_r, r=cur_t, w=bt(s))
                        p.ts("dve", a_sb[:, 0:n], bk[0:64, 0:n], hb_sb[:, 3:4], fb_sb[:, layer:layer + 1], ALU.mult, ALU.add,
                             r=bt(s) + ["hb", "fb"], w=["a"])
                        p.ts("dve", ki_sb[:, 0:n], a_sb[:, 0:n], 1.0 / TWO_PI, None, ALU.mult, r=["a"], w=["ki"])
                        p.stt("dve", r_sb[:, 0:n], ki_sb[:, 0:n], -TWO_PI, a_sb[:, 0:n], ALU.mult, ALU.add, r=["ki", "a"], w=["r"])
                        hh = h_sb[layer % 2]
                        p.actf(hh[:, 0:n], r_sb[:, 0:n], AF.Sin, r=["r"], w=[("h", layer % 2)])
                        if layer < 2:
                            cur_l, cur_r, cur_t = w2_sb[:, layer, :], hh[:, 0:n], ["w2", ("h", layer % 2)]
                        else:
                            cur_l, cur_r, cur_t = w3_sb[:, run_, :], hh[:, 0:n], ["w3", ("h", layer % 2)]
                    p.mm(bk[0:64, 0:n], cur_l, cur_r, r=cur_t, w=bt(s))
                    p.tt("dve", fo_sb[s][:, 0:n], bk[0:64, 0:n], wt_sb[s][:, 0:n], ALU.mult, r=bt(s) + [("wt", s)], w=[("fo", s)])
                    if run_ == 1:
                        nout = n - 1 if t == ntile - 1 else n
                        dst = AP(kf_t, t * n, [[kf_len, 64], [1, nout]])
                    else:
                        nout = n
                        dst = AP(kf_t, L - 1 + t * n, [[kf_len, 64], [1, nout]])
                    p.dma("pool", dst, fo_sb[s][:, 0:nout], r=[("fo", s)], w=[("KF", kf_len, run_, t)])
                    kf_toks.setdefault(kf_len, []).append(("KF", kf_len, run_, t))

        mlp(zf, win, LL, KF, 32768)
        mlp(zfc, winc, 256, KFc, 512)

        zr_sb = p.sb([128, 64, 256], BF16)
        zn_sb = p.sb([128, 64, 256], BF16)
        zrc_sb = p.sb([128, 64, 4], BF16)
        znc_sb = p.sb([128, 64, 4], BF16)
        hbias_sb = p.sb([128, 64], F32)
        p.dma("sp", zr_sb[:], zrev, w=["zr"])
        p.dma("sp", zn_sb[:], znat, w=["zn"])
        p.dma("sp", zrc_sb[:], zrevc, w=["zrc"])
        p.dma("sp", znc_sb[:], znatc, w=["znc"])
        p.dma("sp", hbias_sb[:], hbias, w=["hbias"])
        NS = 6
        strip = [p.sb([128, NLAG * 128], BF16) for _ in range(NS)]
        yst = [p.sb([128, 256], BF16) for _ in range(4)]
        sc = {"s": 0, "y": 0}
        order = [7] + [i for i in range(NPIECE) if i != 7]
        for c in range(64):
            yb = sc["y"] % 2
            psY = B[2 + yb][:, 0:256]
            first = True
            nmm = 0
            for pc in order:
                s = sc["s"] % NS
                sc["s"] += 1
                n0 = pc * NLAG * 128
                src = AP(KF, c * 32768 + n0, [[1, 128], [1, NLAG * 128]])
                q = "sp" if (sc["s"] % 2 == 0) else "act"
                p.dma(q, strip[s][:], src, r=kf_toks[32768], w=[("strip", s)])
                lis = list(range(pc * NLAG, (pc + 1) * NLAG))
                if pc == 7:
                    lis = [127] + [x for x in lis if x != 127]
                for li in lis:
                    d = li - 127
                    I0, I1 = max(0, d), min(128, 128 + d)
                    J0, J1 = I0 - d, I1 - d
                    nmm += 1
                    p.mm(psY[:, 2 * I0:2 * I1], strip[s][:, (li - pc * NLAG) * 128:(li - pc * NLAG + 1) * 128], zr_sb[:, c, 2 * J0:2 * J1],
                         start=first, stop=(nmm == 255), r=[("strip", s), "zr"], w=bt(2 + yb))
                    first = False
            ys = sc["y"] % 4
            sc["y"] += 1
            p.stt("dve", yst[ys][:], zn_sb[:, c, :], hbias_sb[:, c:c + 1], B[2 + yb][:, 0:256], ALU.mult, ALU.add,
                  r=["zn", "hbias"] + bt(2 + yb), w=[("yst", ys)])
            p.dma("pool", o_y[c], yst[ys][:], r=[("yst", ys)], is_out=True)
        stc = [p.sb([128, 384], BF16) for _ in range(2)]
        ystc = [p.sb([128, 4], BF16) for _ in range(2)]
        for c in range(64):
            s = c % 2
            src = AP(KFc, c * 512, [[1, 128], [1, 384]])
            p.dma("sp", stc[s][:], src, r=kf_toks[512], w=[("stc", s)])
            psY = B[4 + s][:, 0:4]
            for idx, d in enumerate((0, -1, 1)):
                li = d + 1
                I0, I1 = max(0, d), min(2, 2 + d)
                J0, J1 = I0 - d, I1 - d
                p.mm(psY[:, 2 * I0:2 * I1], stc[s][:, li * 128:(li + 1) * 128], zrc_sb[:, c, 2 * J0:2 * J1], start=(idx == 0), stop=(idx == 2),
                     r=[("stc", s), "zrc"], w=bt(4 + s))
            p.stt("dve", ystc[s][:], znc_sb[:, c, :], hbias_sb[:, c:c + 1], B[4 + s][:, 0:4], ALU.mult, ALU.add,
                  r=["znc", "hbias"] + bt(4 + s), w=[("ystc", s)])
            p.dma("pool", o_yc[c], ystc[s][:], r=[("ystc", s)], is_out=True)
    return p.finish()


TE = 4608
EPS1 = 1e-6


def build_k3(last=False):
    p = Prog()
    xT = p.dram("xT", [128, 8, TE], F32, "ExternalInput")
    onr = p.dram("onr", [4, 128, TE], BF16, "ExternalInput")
    ong = p.dram("ong", [4, 128, TE], BF16, "ExternalInput")
    sg = p.dram("sg", [8, 128, TE], BF16, "ExternalInput")
    x0 = p.dram("x0", [4, 128, TE], BF16, "ExternalInput")
    yh = p.dram("yh", [4, 128, TE], BF16, "ExternalInput")
    gate = p.dram("gate", [24, 128, TE], BF16, "ExternalInput")
    wbr = p.dram("wbr", [3, 8, 128, 4, 128], BF16, "ExternalInput")
    wout = p.dram("wout", [8, 128, 8, 128], BF16, "ExternalInput")
    wup = p.dram("wup", [44, 128, 8, 128], BF16, "ExternalInput")
    wdn = p.dram("wdn", [8, 128, 22, 128], BF16, "ExternalInput")
    modsel = p.dram("modsel", [128, 48, 2], F32, "ExternalInput")
    n2g = p.dram("n2g", [128, 8], F32, "ExternalInput")
    fconv = p.dram("fconv", [128, 22, 10], F32, "ExternalInput")
    fing = p.dram("fing", [128, 8], F32, "ExternalInput")
    vmlr = p.dram("vmlr", [128, 2], F32, "ExternalInput")
    NOUT = 4096 if last else 4352
    o_x = p.dram("xo", [128, 8, NOUT], F32, "ExternalOutput")

    ones = p.sb([128, 128], BF16)
    p.memset("pool", ones[:], 1.0, w=["ones"])
    epsb = p.sb([128, 1], F32)
    p.memset("pool", epsb[:], EPS1, w=["epsb"])
    ms = p.sb([128, 48, 2], F32)
    g2n = p.sb([128, 8], F32)
    fc_sb = p.sb([128, 22, 10], F32)
    fg_sb = p.sb([128, 8], F32)
    vm = p.sb([128, 2], F32)
    A2 = p.sb([128, 8, 2], F32)
    p.dma("sp", ms[:], modsel, w=["ms"])
    p.dma("sp", g2n[:], n2g, w=["g2n"])
    p.dma("sp", fc_sb[:], fconv, w=["fc"])
    p.dma("sp", fg_sb[:], fing, w=["fg"])
    p.dma("sp", vm[:], vmlr, w=["vm"])
    p.ts("dve", A2[:], ms[:, 32:40, :], 1.0, None, ALU.add, r=["ms"], w=["A2"])
    for r_ in range(2):
        p.tt("dve", A2[:, :, r_], A2[:, :, r_], g2n[:], ALU.mult, r=["A2", "g2n"], w=["A2"])

    NX1 = 2
    x1T = [p.sb([128, 8, 512], F32, f"x1T{i}") for i in range(NX1)]
    h2T = [p.sb([128, 8, 512], BF16, f"h2T{i}") for i in range(NX1)]
    inA = [p.sb([128, 512], BF16) for _ in range(4)]
    inB = [p.sb([128, 512], BF16) for _ in range(4)]
    bT = p.sb([128, 12, 512], BF16)
    gt = [p.sb([128, 3, 512], BF16) for _ in range(2)]
    xin = [p.sb([128, 512], F32) for _ in range(2)]
    mixT = p.sb([128, 8, 512], BF16)
    t1 = p.sb([128, 512], F32)
    t2 = p.sb([128, 512], F32)
    t3 = p.sb([128, 512], F32)
    sq = p.sb([128, 8, 512], BF16)
    rstd = p.sb([128, 512], F32)
    xn = [p.sb([128, 512], F32) for _ in range(2)]
    wbr_sb = [p.sb([128, 4, 128], BF16) for _ in range(4)]
    wout_sb = [p.sb([128, 8, 128], BF16) for _ in range(2)]
    wup_sb = [p.sb([128, 8, 128], BF16) for _ in range(4)]
    wdn_sb = [p.sb([128, 22, 128], BF16) for _ in range(2)]
    a_sb = [p.sb([128, 640], F32) for _ in range(2)]
    acc = [p.sb([128, 512], F32) for _ in range(2)]
    gl = [p.sb([128, 512], F32) for _ in range(2)]
    uT = p.sb([128, 22, 512], BF16)
    xo = p.sb([128, 8, 512], F32)
    B = [p.ps([128, 512], F32, f"B{i}") for i in range(8)]
    cn = {"in": 0, "gt": 0, "xin": 0, "wbr": 0, "wout": 0, "wup": 0, "wdn": 0, "a": 0, "ps": 0}

    def bk(i):
        return ("B", i)

    def P1(t):
        c0 = t * 512
        cols = slice(c0, c0 + 512)
        slot = t % NX1
        srcs = [(onr, j, sg, j) for j in range(4)] + [(ong, j, sg, 4 + j) for j in range(4)] + [(x0, j, yh, j) for j in range(4)]
        for bi, (sa, ja, sb_, jb) in enumerate(srcs):
            s = cn["in"] % 4
            cn["in"] += 1
            p.dma("sp", inA[s][:], sa[ja, :, cols], w=[("inA", s)])
            p.dma("sp", inB[s][:], sb_[jb, :, cols], w=[("inB", s)])
            p.tt("pool", bT[:, bi, :], inA[s][:], inB[s][:], ALU.mult, r=[("inA", s), ("inB", s)], w=[("bT", bi)])
        for oc in range(8):
            gs = cn["gt"] % 2
            cn["gt"] += 1
            for br in range(3):
                p.dma("sp", gt[gs][:, br, :], gate[br * 8 + oc, :, cols], w=[("gt", gs)])
            pss = []
            for br in range(3):
                ws = cn["wbr"] % 4
                cn["wbr"] += 1
                p.dma("act", wbr_sb[ws][:], wbr[br, oc], w=[("wbr", ws)])
                pb = cn["ps"] % 6
                cn["ps"] += 1
                for kc in range(4):
                    p.mm(B[pb][:], wbr_sb[ws][:, kc, :], bT[:, br * 4 + kc, :], start=(kc == 0), stop=(kc == 3),
                         r=[("wbr", ws), ("bT", br * 4 + kc)], w=[bk(pb)])
                pss.append(pb)
            p.tt("dve", t1[:], B[pss[0]][:], gt[gs][:, 0, :], ALU.mult, r=[bk(pss[0]), ("gt", gs)], w=["t1"])
            p.tt("dve", t2[:], B[pss[1]][:], gt[gs][:, 1, :], ALU.mult, r=[bk(pss[1]), ("gt", gs)], w=["t2"])
            p.tt("dve", t3[:], B[pss[2]][:], gt[gs][:, 2, :], ALU.mult, r=[bk(pss[2]), ("gt", gs)], w=["t3"])
            p.tt("pool", t1[:], t1[:], t2[:], ALU.add, r=["t1", "t2"], w=["t1"])
            p.tt("pool", mixT[:, oc, :], t1[:], t3[:], ALU.add, r=["t1", "t3"], w=[("mix", oc)])
        for oc in range(8):
            ws = cn["wout"] % 2
            cn["wout"] += 1
            p.dma("act", wout_sb[ws][:], wout[oc], w=[("wout", ws)])
            pb = cn["ps"] % 6
            cn["ps"] += 1
            for k in range(8):
                p.mm(B[pb][:], wout_sb[ws][:, k, :], mixT[:, k, :], start=(k == 0), stop=(k == 7),
                     r=[("wout", ws), ("mix", k)], w=[bk(pb)])
            xs = cn["xin"] % 2
            cn["xin"] += 1
            p.dma("sp", xin[xs][:], xT[:, oc, cols], w=[("xin", xs)])
            if t < 8:
                p.stt("dve", x1T[slot][:, oc, :], B[pb][:], ms[:, 16 + oc, 0:1], xin[xs][:], ALU.mult, ALU.add,
                      r=[bk(pb), "ms", ("xin", xs)], w=[("x1", slot)])
            else:
                for hf, r_ in ((0, 0), (1, 1)):
                    hs = slice(hf * 256, hf * 256 + 256)
                    p.stt("dve", x1T[slot][:, oc, hs], B[pb][:, hs], ms[:, 16 + oc, r_:r_ + 1], xin[xs][:, hs], ALU.mult, ALU.add,
                          r=[bk(pb), "ms", ("xin", xs)], w=[("x1", slot)])
        p.actf(sq[:], x1T[slot][:], AF.Square, r=[("x1", slot)], w=["sq"])
        for k in range(8):
            p.mm(B[6][:], ones[:], sq[:, k, :], start=(k == 0), stop=(k == 7), r=["ones", "sq"], w=[bk(6)])
        p.actf(rstd[:], B[6][:], AF.Sqrt, r=[bk(6), "epsb"], w=["rstd"], scale=1.0 / 1024.0, bias=epsb[:])
        p.add("dve", lambda e: e.reciprocal(rstd[:], rstd[:]), r=["rstd"], w=["rstd"])
        for k in range(8):
            xk = xn[k % 2]
            xkt = ("xn", k % 2)
            p.tt("dve", xk[:], x1T[slot][:, k, :], rstd[:], ALU.mult, r=[("x1", slot), "rstd"], w=[xkt])
            halves = [(slice(0, 512), 0)] if t < 8 else [(slice(0, 256), 0), (slice(256, 512), 1)]
            for (hs, r_) in halves:
                if k % 2 == 0:
                    p.actf(h2T[slot][:, k, hs], xk[:, hs], AF.Identity, r=[xkt, "A2", "ms"], w=[("h2", slot)],
                           scale=A2[:, k, r_:r_ + 1], bias=ms[:, 24 + k, r_:r_ + 1])
                else:
                    p.ts("dve", h2T[slot][:, k, hs], xk[:, hs], A2[:, k, r_:r_ + 1], ms[:, 24 + k, r_:r_ + 1], ALU.mult, ALU.add,
                         r=[xkt, "A2", "ms"], w=[("h2", slot)])
        if t == 0:
            p.ts("dve", h2T[slot][:, :, 0:128], h2T[slot][:, :, 0:128], vm[:, 0:1], None, ALU.mult, r=["vm", ("h2", slot)], w=[("h2", slot)])
        if t == 8:
            p.ts("dve", h2T[slot][:, :, 128:256], h2T[slot][:, :, 128:256], vm[:, 1:2], None, ALU.mult, r=["vm", ("h2", slot)], w=[("h2", slot)])

    def hseg(c_lo, c_hi):
        out = []
        c = c_lo
        while c < c_hi:
            t = c // 512
            e = min(c_hi, (t + 1) * 512)
            out.append((t % NX1, slice(c - t * 512, e - t * 512), e - c))
            c = e
        return out

    def P2(u, ctx=False):
        if not ctx:
            o0 = 128 + 512 * u
            NO = 512
            a0, NA = o0 - 64, 640
            r_ = 0
        else:
            o0, NO, a0, NA, r_ = 4352, 256, 4352, 256, 1
        apieces = hseg(a0, a0 + NA)
        opieces = hseg(o0, o0 + NO)
        for fc in range(22):
            ws = cn["wup"] % 4
            cn["wup"] += 1
            p.dma("act", wup_sb[ws][:], wup[fc], w=[("wup", ws)])
            asl = cn["a"] % 2
            cn["a"] += 1
            off = 0
            for (sl, lsl, n) in apieces:
                pb = cn["ps"] % 6
                cn["ps"] += 1
                for k in range(8):
                    p.mm(B[pb][:, 0:n], wup_sb[ws][:, k, :], h2T[sl][:, k, lsl], start=(k == 0), stop=(k == 7),
                         r=[("wup", ws), ("h2", sl)], w=[bk(pb)])
                p.cp("act", a_sb[asl][:, off:off + n], B[pb][:, 0:n], r=[bk(pb)], w=[("a", asl)])
                off += n
            ws2 = cn["wup"] % 4
            cn["wup"] += 1
            p.dma("act", wup_sb[ws2][:], wup[22 + fc], w=[("wup", ws2)])
            pv = cn["ps"] % 6
            cn["ps"] += 1
            off = 0
            for (sl, lsl, n) in opieces:
                for k in range(8):
                    p.mm(B[pv][:, off:off + n], wup_sb[ws2][:, k, :], h2T[sl][:, k, lsl], start=(k == 0), stop=(k == 7),
                         r=[("wup", ws2), ("h2", sl)], w=[bk(pv)])
                off += n
            ac = acc[asl]
            at = ("acc", asl)
            A = a_sb[asl]
            w = lambda i, j: fc_sb[:, fc, i * 3 + j:i * 3 + j + 1]
            if not ctx:
                a3 = A[:, 0:640].rearrange("p (r c) -> p r c", c=64)
                o3 = ac[:, :].rearrange("p (r c) -> p r c", c=64)
                p.ts("dve", ac[:], A[:, 64:576], w(1, 1), fc_sb[:, fc, 9:10], ALU.mult, ALU.add, r=[("a", asl), "fc"], w=[at])
                for i in range(3):
                    for j in range(3):
                        if i == 1 and j == 1:
                            continue
                        if j == 1:
                            src, dst = a3[:, i:i + 8, :], o3
                        elif j == 0:
                            src, dst = a3[:, i:i + 8, 0:63], o3[:, :, 1:64]
                        else:
                            src, dst = a3[:, i:i + 8, 1:64], o3[:, :, 0:63]
                        p.stt("dve", dst, src, w(i, j), dst, ALU.mult, ALU.add, r=[("a", asl), "fc", at], w=[at])
            else:
                p.ts("dve", ac[:, 0:256], A[:, 0:256], w(1, 1), fc_sb[:, fc, 9:10], ALU.mult, ALU.add, r=[("a", asl), "fc"], w=[at])
                p.stt("dve", ac[:, 1:256], A[:, 0:255], w(1, 0), ac[:, 1:256], ALU.mult, ALU.add, r=[("a", asl), "fc", at], w=[at])
                p.stt("dve", ac[:, 0:255], A[:, 1:256], w(1, 2), ac[:, 0:255], ALU.mult, ALU.add, r=[("a", asl), "fc", at], w=[at])
            p.actf(gl[asl][:, 0:NO], ac[:, 0:NO], AF.Gelu, r=[at], w=[("gl", asl)])
            p.tt("dve", uT[:, fc, 0:NO], gl[asl][:, 0:NO], B[pv][:, 0:NO], ALU.mult, r=[("gl", asl), bk(pv)], w=[("uT", fc)])
        for oc in range(8):
            ws = cn["wdn"] % 2
            cn["wdn"] += 1
            p.dma("act", wdn_sb[ws][:], wdn[oc], w=[("wdn", ws)])
            pb = cn["ps"] % 6
            cn["ps"] += 1
            for fc in range(22):
                p.mm(B[pb][:, 0:NO], wdn_sb[ws][:, fc, :], uT[:, fc, 0:NO], start=(fc == 0), stop=(fc == 21),
                     r=[("wdn", ws), ("uT", fc)], w=[bk(pb)])
            off = 0
            for (sl, lsl, n) in opieces:
                p.stt("dve", xo[:, oc, off:off + n], B[pb][:, off:off + n], ms[:, 40 + oc, r_:r_ + 1], x1T[sl][:, oc, lsl], ALU.mult, ALU.add,
                      r=[bk(pb), "ms", ("x1", sl)], w=["xo"])
                off += n
        if last:
            p.actf(sq[:, :, 0:NO], xo[:, :, 0:NO], AF.Square, r=["xo"], w=["sq"])
            for k in range(8):
                p.mm(B[7][:, 0:NO], ones[:], sq[:, k, 0:NO], start=(k == 0), stop=(k == 7), r=["ones", "sq"], w=[bk(7)])
            p.actf(rstd[:, 0:NO], B[7][:, 0:NO], AF.Sqrt, r=[bk(7), "epsb"], w=["rstd"], scale=1.0 / 1024.0, bias=epsb[:])
            p.add("dve", lambda e: e.reciprocal(rstd[:, 0:NO], rstd[:, 0:NO]), r=["rstd"], w=["rstd"])
            p.tt("dve", xo[:, :, 0:NO], xo[:, :, 0:NO], rstd[:, 0:NO].unsqueeze(1).broadcast_to([128, 8, NO]), ALU.mult,
                 r=["xo", "rstd"], w=["xo"])
            for k in range(8):
                p.ts("dve", xo[:, k, 0:NO], xo[:, k, 0:NO], fg_sb[:, k:k + 1], None, ALU.mult, r=["xo", "fg"], w=["xo"])
        oo = 512 * u if not ctx else 4096
        p.dma("pool", o_x[:, :, oo:oo + NO], xo[:, :, 0:NO], r=["xo"], is_out=True)

    P1(0)
    for t in range(1, 9):
        P1(t)
        P2(t - 1)
    if not last:
        P2(0, ctx=True)
    return p.finish()

BF = ml_dtypes.bfloat16
RET_F = [math.log1p(-2.0 ** (-5.0 - h)) for h in range(4)]
RET_B = [math.log1p(-2.0 ** (-5.5 - h)) for h in range(4)]
W_SHAPES = [("w_in", (1024, 7712)), ("w_branch", (3, 512, 1024)), ("w_out", (1024, 1024)), ("w_up", (1024, 5632)), ("w_down", (2816, 1024))]


def _to_pk(v, nchunk):
    return np.ascontiguousarray(np.moveaxis(v.reshape(v.shape[:-1] + (nchunk, 128)), -1, 0))


def _feat_major(xe):
    T, D = xe.shape
    return np.ascontiguousarray(xe.T.reshape(D // 128, 128, T).transpose(1, 0, 2))


def _ext_tokens(xl, xc, s):
    D = xl.shape[1]
    out = np.zeros((TE, D), xl.dtype)
    lo = s * 4096 - 128
    a, b_ = max(lo, 0), min(lo + 4352, 16384)
    out[a - lo:b_ - lo] = xl[a:b_]
    out[4352:] = xc
    return out


def _ext_fm(full_lat, full_ctx, s):
    C = full_lat.shape[0]
    out = np.zeros((C, TE), full_lat.dtype)
    lo = s * 4096 - 128
    a, b_ = max(lo, 0), min(lo + 4352, 16384)
    out[:, a - lo:b_ - lo] = full_lat[:, a:b_]
    out[:, 4352:] = full_ctx
    return out


def _k2_consts():
    u = np.arange(128)[:, None]
    t = np.arange(128)[None, :]
    tri = np.zeros((128, 4, 128), np.float32)
    tri[:, 0] = np.where(u <= t, -1 / 16, 0)
    tri[:, 1] = np.where(u >= t, -1 / 16, 0)
    tri[:, 2] = np.where(u > t, -1 / 16, 0)
    tri[:, 3] = np.where(u < t, -1 / 16, 0)
    msk = np.zeros((128, 2, 128), np.float32)
    msk[:, 0] = (u <= t)
    msk[:, 1] = (u > t)
    return tri, msk


def _pos_feats(L):
    t = np.linspace(0.0, 1.0, L, dtype=np.float32)[:, None]
    ang = (np.float32(2.0 * math.pi / L) * np.arange(L, dtype=np.float32)[:, None]) * np.linspace(1e-4, 16 - 1.0, 16, dtype=np.float32)[None, :]
    ang = ang.astype(np.float64)
    return np.concatenate([t.astype(np.float64), np.cos(ang), -np.sin(ang)], -1)


def _window(L, chans):
    t = np.linspace(0.0, 1.0, L, dtype=np.float32)[:, None].astype(np.float64)
    mn = math.log(1e-2) / 1.5
    mx = math.log(1e-2) / 0.3
    deltas = np.abs(np.linspace(mn, mx, 512, dtype=np.float32)).astype(np.float64)[chans]
    return np.exp(-t * deltas[None, :]) + 0.05


def _hy_tables(L, chans):
    z = _pos_feats(L)
    w = _window(L, chans)
    zf = np.stack([z.T, z[::-1].T], 0).astype(np.float32)
    wn = np.stack([w.T, w[::-1].T], 0).astype(np.float32)
    return np.ascontiguousarray(zf), np.ascontiguousarray(wn)


def _lay_w(w):
    K, M = w.shape
    return np.ascontiguousarray(w.reshape(K // 128, 128, M // 128, 128).transpose(2, 1, 0, 3))


_NC_CACHE = {}


def _get_nc(name):
    if name not in _NC_CACHE:
        _NC_CACHE[name] = {"k0": build_k0, "k1": build_k1, "k2": build_k2, "k3": lambda: build_k3(False), "k3l": lambda: build_k3(True)}[name]()
    return _NC_CACHE[name]


def _run(name, ims):
    nc = _get_nc(name)
    res = run_bass_kernel_spmd(nc, ims, core_ids=list(range(8)))
    return res.results


def kernel(**inp):
    inp = {k: np.asarray(v) for k, v in inp.items()}
    f32 = np.float32
    flat = np.concatenate([np.ascontiguousarray(inp[n][l], dtype=f32).ravel() for l in range(2) for n, _ in W_SHAPES])
    NPC = flat.size // 8
    C3 = np.concatenate([inp["c"].astype(f32), inp["c_ctx"].astype(f32)[None]], 0)
    cT = np.ascontiguousarray(_to_pk(C3, 8).transpose(0, 2, 1))
    ims = []
    for i in range(8):
        adaw = np.ascontiguousarray(inp["ada_w"][:, :, i * 768:(i + 1) * 768], dtype=f32)
        adab = np.ascontiguousarray(inp["ada_b"][:, i * 768:(i + 1) * 768].reshape(2, 6, 128).transpose(2, 0, 1), dtype=f32)
        ims.append({"wsl": flat[i * NPC:(i + 1) * NPC].reshape(128, NWP), "adaw": adaw, "cT": cT, "adab": adab})
    r0 = _run("k0", ims)
    wflat = np.concatenate([np.asarray(r0[i]["wbf"]).reshape(-1) for i in range(8)])
    Wb = []
    off = 0
    for l in range(2):
        d = {}
        for n, shp in W_SHAPES:
            sz = int(np.prod(shp))
            d[n] = wflat[off:off + sz].reshape(shp)
            off += sz
        Wb.append(d)
    mods = []
    for l in range(2):
        m = np.zeros((3, 6144), f32)
        for i in range(8):
            mt = np.asarray(r0[i]["modT"])
            m[:, i * 768:(i + 1) * 768] = mt[:, l].transpose(2, 1, 0).reshape(3, 768)
        mods.append(m)

    tri, msk = _k2_consts()
    tabs = [(_hy_tables(LL, np.arange(64 * j, 64 * j + 64)), _hy_tables(256, np.arange(64 * j, 64 * j + 64))) for j in range(8)]

    x_lat = inp["x"].astype(f32)
    x_ctx = inp["ctx"].astype(f32)
    out = None
    for l in range(2):
        last = (l == 1)
        mod = mods[l]
        hs = np.concatenate([inp["hy_short_w"][l], inp["hy_short_b"][l][None]], 0).astype(f32)
        hsw = np.ascontiguousarray(_to_pk(hs, 12).transpose(0, 2, 1))
        xTs, modsels, vmlrs = [], [], []
        ims = []
        for i in range(8):
            b, s = i // 4, i % 4
            xTs.append(_feat_major(_ext_tokens(x_lat[b], x_ctx[b], s)))
            msl = np.stack([mod[b], mod[2]], 0)
            modsels.append(np.ascontiguousarray(_to_pk(msl, 48).transpose(0, 2, 1)).astype(f32))
            vmlrs.append(np.tile(np.array([[0.0 if s == 0 else 1.0, 0.0 if s == 3 else 1.0]], f32), (128, 1)))
            ims.append({"xT": xTs[i], "win": Wb[l]["w_in"], "modsel": modsels[i], "n1g": _to_pk(inp["norm1_g"][l].astype(f32), 8),
                        "hsw": hsw, "vmlr": vmlrs[i]})
        r1 = [{k: np.asarray(v) for k, v in r.items()} for r in _run("k1", ims)]
        def full_fm(key, b):
            lat = np.concatenate([r1[b * 4 + s][key][:, :, 128:4224] for s in range(4)], -1)
            ctxp = r1[b * 4][key][:, :, 4352:4608]
            return lat.reshape(-1, 16384), ctxp.reshape(-1, 256)
        qk_f = [full_fm("qk", b) for b in range(2)]
        z_f = [full_fm("z", b) for b in range(2)]
        kv_f = []
        lr_f = []
        for b in range(2):
            kv_f.append((np.concatenate([r1[b * 4 + s]["kv"][128:4224] for s in range(4)], 0), r1[b * 4]["kv"][4352:4608]))
            lr_f.append((np.concatenate([r1[b * 4 + s]["lrT"][:, 128:4224] for s in range(4)], -1), r1[b * 4]["lrT"][:, 4352:4608]))
        ims = []
        for j in range(8):
            b, h = j // 4, j % 4
            d = {}
            qkl, qkc = qk_f[b]
            qkall = np.concatenate([qkc, qkl], -1)
            rows = lambda base: slice(base + h * 64, base + h * 64 + 64)
            d["qT"] = np.ascontiguousarray(np.stack([qkall[rows(0)], qkall[rows(512)]], 0))
            d["kT"] = np.ascontiguousarray(np.stack([qkall[rows(256)], qkall[rows(768)]], 0))
            kvl, kvc = kv_f[b]
            kvall = np.concatenate([kvc, kvl], 0)
            d["ktok"] = np.ascontiguousarray(np.stack([kvall[:, h * 64:h * 64 + 64], kvall[:, 768 + h * 64:768 + h * 64 + 64]], 0))
            d["vtok"] = np.ascontiguousarray(np.stack([kvall[:, 256 + h * 128:256 + h * 128 + 128], kvall[:, 1024 + h * 128:1024 + h * 128 + 128]], 0))
            lrl, lrc = lr_f[b]
            lrall = np.concatenate([lrc, lrl], -1)
            lr34 = np.ones((34, TT), f32)
            lr34[0:16] = lrall[0:16]
            lr34[17:33] = lrall[16:32]
            d["lrT"] = lr34
            wa = np.zeros((34, 128), f32)
            wa2 = inp["gla_wa2"][l].astype(f32)
            ba = inp["gla_ba"][l].astype(f32)
            wa[0:16, :64] = wa2[0][:, h * 64:h * 64 + 64]
            wa[16, :64] = ba[0][h * 64:h * 64 + 64]
            wa[17:33, 64:] = wa2[1][:, h * 64:h * 64 + 64]
            wa[33, 64:] = ba[1][h * 64:h * 64 + 64]
            d["wa"] = wa
            laR = np.zeros((128, 2, 64), f32)
            laR[:, 0, :] = -16 * RET_F[h]
            laR[:, 1, :] = -16 * RET_B[h]
            d["laR"] = laR
            d["tri"] = tri
            d["msk"] = msk
            (zfL, wnL), (zfC, wnC) = tabs[j]
            d["zf"], d["win"], d["zfc"], d["winc"] = zfL, wnL, zfC, wnC
            d["hw1"] = np.ascontiguousarray(inp["hy_w1"][l], dtype=f32)
            d["hb"] = np.ascontiguousarray(np.stack([inp["hy_b1"][l], inp["hy_b2"][l][0], inp["hy_b2"][l][1], inp["hy_freq"][l]], 1), dtype=f32)
            d["hw2"] = np.ascontiguousarray(inp["hy_w2"][l].transpose(1, 0, 2), dtype=f32)
            w3 = inp["hy_w3"][l].astype(f32)
            d["hw3"] = np.ascontiguousarray(np.stack([w3[:, 64 * j:64 * j + 64], w3[:, 512 + 64 * j:512 + 64 * j + 64]], 1))
            ch = slice(64 * j, 64 * j + 64)
            zl = np.stack([z_f[bb][0][ch] for bb in range(2)], 0)
            zc = np.stack([z_f[bb][1][ch] for bb in range(2)], 0)
            def lay(zz, nb, rev):
                a = zz.reshape(2, 64, nb, 128)
                if rev:
                    a = a[:, :, :, ::-1]
                return np.ascontiguousarray(a.transpose(3, 1, 2, 0)).reshape(128, 64, 2 * nb)
            d["zrev"], d["znat"] = lay(zl, 128, True), lay(zl, 128, False)
            d["zrevc"], d["znatc"] = lay(zc, 2, True), lay(zc, 2, False)
            d["hbias"] = np.ascontiguousarray(np.tile(inp["hy_bias"][l][ch][None].astype(f32), (128, 1)))
            ims.append(d)
        r2 = [{k: np.asarray(v) for k, v in r.items()} for r in _run("k2", ims)]
        y_lat = np.zeros((2, 512, 16384), BF)
        y_ctx = np.zeros((2, 512, 256), BF)
        for j in range(8):
            Y = r2[j]["Y"].reshape(64, 128, 128, 2)
            y_lat[:, 64 * j:64 * j + 64] = Y.transpose(3, 0, 2, 1).reshape(2, 64, 16384)
            Yc = r2[j]["Yc"].reshape(64, 128, 2, 2)
            y_ctx[:, 64 * j:64 * j + 64] = Yc.transpose(3, 0, 2, 1).reshape(2, 64, 256)
        wbr_l = np.stack([_lay_w(Wb[l]["w_branch"][g]) for g in range(3)], 0)
        wout_l = _lay_w(Wb[l]["w_out"])
        wup_l = _lay_w(Wb[l]["w_up"])
        wdn_l = _lay_w(Wb[l]["w_down"])
        fconv = np.concatenate([inp["ffn_conv_w"][l].reshape(9, 2816), inp["ffn_conv_b"][l][None]], 0).astype(f32)
        fconv = np.ascontiguousarray(_to_pk(fconv, 22).transpose(0, 2, 1))
        ims = []
        for i in range(8):
            b, s = i // 4, i % 4
            d = {"xT": xTs[i], "sg": r1[i]["sg"], "x0": r1[i]["x0"], "gate": r1[i]["gate"]}
            for key, g in (("onr", 0), ("ong", 1)):
                on_b = np.stack([r2[b * 4 + h]["onT"][g] for h in range(4)], 0)
                d[key] = _ext_fm(on_b[:, :, 256:].reshape(512, 16384), on_b[:, :, :256].reshape(512, 256), s).reshape(4, 128, TE)
            d["yh"] = _ext_fm(y_lat[b], y_ctx[b], s).reshape(4, 128, TE)
            d["wbr"], d["wout"], d["wup"], d["wdn"] = wbr_l, wout_l, wup_l, wdn_l
            d["modsel"] = modsels[i]
            d["n2g"] = _to_pk(inp["norm2_g"][l].astype(f32), 8)
            d["fconv"] = fconv
            d["fing"] = _to_pk(inp["final_g"].astype(f32), 8)
            d["vmlr"] = vmlrs[i]
            ims.append(d)
        r3 = _run("k3l" if last else "k3", ims)
        xo = [np.asarray(r["xo"]) for r in r3]
        new_lat = np.zeros((2, 16384, 1024), f32)
        for i in range(8):
            b, s = i // 4, i % 4
            new_lat[b, s * 4096:(s + 1) * 4096] = xo[i][:, :, :4096].transpose(2, 1, 0).reshape(4096, 1024)
        if not last:
            x_ctx = np.stack([xo[b * 4][:, :, 4096:].transpose(2, 1, 0).reshape(256, 1024) for b in range(2)], 0)
        x_lat = new_lat
    return x_lat
```

```python
import contextlib
import math
import os
import numpy as np
import concourse.bass as bass
import concourse.mybir as mybir
from concourse.ap import AP
from concourse.bass_utils import run_bass_kernel_spmd
import ml_dtypes

F32 = mybir.dt.float32
BF16 = mybir.dt.bfloat16
AF = mybir.ActivationFunctionType
ALU = mybir.AluOpType
AX = mybir.AxisListType

ENGS = ("pe", "act", "dve", "pool", "sp")
NDMASEM = 24
DMAQ = ("sp", "act", "pool")


class _Op:
    __slots__ = ("eng", "fn", "deps", "isdma", "needed", "semval", "dsem", "dval", "prevdma")

    def __init__(self, eng, fn, isdma):
        self.eng = eng
        self.fn = fn
        self.deps = []
        self.isdma = isdma
        self.needed = False
        self.semval = 0
        self.dsem = -1
        self.dval = 0
        self.prevdma = None


class Prog:
    def __init__(self):
        self.nc = bass.Bass("TRN2", target_bir_lowering=False)
        self.streams = {e: [] for e in ENGS}
        self.tok = {}
        self.es = contextlib.ExitStack()
        self.ndma = {q: 0 for q in DMAQ}
        self.dma_last = {q: [None] * NDMASEM for q in DMAQ}
        self.dma_cnt = {q: [0] * NDMASEM for q in DMAQ}
        self.out_dmas = []
        self._n = 0

    def dram(self, name, shape, dt, kind):
        return self.nc.dram_tensor(name, list(shape), dt, kind=kind).ap()

    def sb(self, shape, dt, name=None):
        self._n += 1
        return self.es.enter_context(self.nc.sbuf_tensor(name or f"sb{self._n}", list(shape), dt))

    def ps(self, shape, dt=F32, name=None):
        self._n += 1
        return self.es.enter_context(self.nc.psum_tensor(name or f"ps{self._n}", list(shape), dt))

    def _deps(self, op, r, w):
        deps = op.deps
        for t in r:
            st = self.tok.get(t)
            if st is None:
                st = self.tok[t] = [None, []]
            if st[0] is not None:
                deps.append(st[0])
        for t in w:
            st = self.tok.get(t)
            if st is None:
                st = self.tok[t] = [None, []]
            if st[0] is not None:
                deps.append(st[0])
            deps.extend(st[1])
        for t in r:
            self.tok[t][1].append(op)
        for t in w:
            st = self.tok[t]
            st[0] = op
            st[1] = []

    def add(self, eng, fn, r=(), w=()):
        op = _Op(eng, fn, False)
        self._deps(op, r, w)
        self.streams[eng].append(op)
        return op

    def dma(self, q, out, in_, r=(), w=(), is_out=False, **kw):
        op = _Op(q, (lambda e, out=out, in_=in_, kw=kw: e.dma_start(out=out, in_=in_, **kw)), True)
        self._deps(op, r, w)
        s = self.ndma[q] % NDMASEM
        self.ndma[q] += 1
        op.dsem = (q, s)
        self.dma_cnt[q][s] += 1
        op.dval = 16 * self.dma_cnt[q][s]
        op.prevdma = self.dma_last[q][s]
        self.dma_last[q][s] = op
        op.needed = True
        self.streams[q].append(op)
        if is_out:
            self.out_dmas.append(op)
        return op

    def coll(self, kind, ins, outs, r=(), w=(), groups=None):
        groups = groups or [list(range(8))]
        op = _Op("pool", (lambda e: e.collective_compute(kind, ALU.bypass, groups, [a.opt() for a in ins], [a.opt() for a in outs])), True)
        self._deps(op, r, w)
        self.ncoll = getattr(self, "ncoll", 0) + 1
        op.dsem = ("cc", 0)
        op.dval = self.ncoll
        op.prevdma = None
        op.needed = True
        self.streams["pool"].append(op)
        return op

    def mm(self, out, lhsT, rhs, start=True, stop=True, r=(), w=(), **kw):
        return self.add("pe", lambda e: e.matmul(out, lhsT, rhs, start=start, stop=stop, **kw), r, w)

    def actf(self, out, in_, func, r=(), w=(), eng="act", **kw):
        return self.add(eng, lambda e: e.activation(out, in_, func, **kw), r, w)

    def tt(self, eng, out, a, b, op, r=(), w=()):
        return self.add(eng, lambda e: e.tensor_tensor(out, a, b, op), r, w)

    def ts(self, eng, out, a, s1, s2, op0, op1=None, r=(), w=()):
        if op1 is None:
            return self.add(eng, lambda e: e.tensor_scalar(out, a, s1, s2, op0), r, w)
        return self.add(eng, lambda e: e.tensor_scalar(out, a, s1, s2, op0, op1), r, w)

    def stt(self, eng, out, a, s, b, op0, op1, r=(), w=()):
        return self.add(eng, lambda e: e.scalar_tensor_tensor(out, a, s, b, op0, op1), r, w)

    def cp(self, eng, out, in_, r=(), w=()):
        if eng == "act":
            return self.add(eng, lambda e: e.copy(out, in_), r, w)
        return self.add(eng, lambda e: e.tensor_copy(out, in_), r, w)

    def memset(self, eng, ap, val, w=()):
        return self.add(eng, lambda e: e.memset(ap, val), (), w)

    def finish(self):
        nc = self.nc
        for e in ENGS:
            for op in self.streams[e]:
                for d in op.deps:
                    if not d.isdma:
                        if d.eng == "pe" and op.eng == "pe" and not op.isdma:
                            continue
                        d.needed = True
        for e in ENGS:
            c = 0
            for op in self.streams[e]:
                if not op.isdma and op.needed:
                    c += 1
                    op.semval = c
        csem = {e: self.es.enter_context(nc.semaphore(f"c_{e}")) for e in ("pe", "act", "dve", "pool")}
        dsem = {(q, i): self.es.enter_context(nc.semaphore(f"d_{q}_{i}")) for q in DMAQ for i in range(NDMASEM)}
        dsem[("cc", 0)] = self.es.enter_context(nc.semaphore("cc_sem"))
        streams = self.streams
        out_dmas = self.out_dmas

        def emit(eng_name, e):
            waited = {}

            def wait(key, sem, val):
                if waited.get(key, 0) >= val:
                    return
                waited[key] = val
                e.wait_ge(sem, val)

            def wait_op(d):
                if d.isdma:
                    wait(("d", d.dsem), dsem[d.dsem], d.dval)
                else:
                    wait(("c", d.eng), csem[d.eng], d.semval)

            for op in streams[eng_name]:
                for d in op.deps:
                    if (not d.isdma) and d.eng == "pe" and eng_name == "pe" and not op.isdma:
                        continue
                    wait_op(d)
                if op.isdma and op.prevdma is not None:
                    wait_op(op.prevdma)
                ins = op.fn(e)
                if op.isdma:
                    ins.then_inc(dsem[op.dsem], 1 if op.dsem[0] == "cc" else 16)
                elif op.needed:
                    ins.then_inc(csem[eng_name], 1)
            if eng_name == "sp":
                for d in out_dmas:
                    wait_op(d)

        with nc.Block() as block:
            @block.sync
            def _(e):
                emit("sp", e)

            @block.tensor
            def _(e):
                emit("pe", e)

            @block.scalar
            def _(e):
                emit("act", e)

            @block.vector
            def _(e):
                emit("dve", e)

            @block.gpsimd
            def _(e):
                emit("pool", e)
        self.es.close()
        return nc


NWP = 37440
NPIECE0 = 10
PW0 = NWP // NPIECE0

def build_k0():
    p = Prog()
    wsl = p.dram("wsl", [128, NWP], F32, "ExternalInput")
    adaw = p.dram("adaw", [2, 1024, 768], F32, "ExternalInput")
    cT = p.dram("cT", [128, 8, 3], F32, "ExternalInput")
    adab = p.dram("adab", [128, 2, 6], F32, "ExternalInput")
    wbf = p.dram("wbf", [128, NWP], BF16, "ExternalOutput")
    modT = p.dram("modT", [128, 2, 6, 3], F32, "ExternalOutput")

    c_sb = p.sb([128, 8, 3], F32)
    s_sb = p.sb([128, 8, 3], F32)
    b_sb = p.sb([128, 2, 6], F32)
    aw = p.sb([128, 2, 8, 768], F32)
    m_sb = p.sb([128, 2, 6, 3], F32)
    pm = p.ps([128, 2, 6, 4], F32)
    p.dma("sp", c_sb[:], cT, w=["c"])
    p.dma("sp", b_sb[:], adab, w=["b"])
    for l in range(2):
        for k in range(8):
            p.dma("sp", aw[:, l, k, :], adaw[l, k * 128:(k + 1) * 128, :], w=[("aw", l, k)])
    p.actf(s_sb[:], c_sb[:], AF.Silu, r=["c"], w=["s"])
    for l in range(2):
        for j in range(6):
            for k in range(8):
                p.mm(pm[:, l, j, 0:3], aw[:, l, k, j * 128:(j + 1) * 128], s_sb[:, k, :],
                     start=(k == 0), stop=(k == 7), r=[("aw", l, k), "s"], w=["pm"])
    for l in range(2):
        for j in range(6):
            p.ts("dve", m_sb[:, l, j, :], pm[:, l, j, 0:3], b_sb[:, l, j:j + 1], None, ALU.add,
                 r=["pm", "b"], w=["m"])
    p.dma("pool", modT, m_sb[:], r=["m"], is_out=True)

    NB = 3
    fin = [p.sb([128, PW0], F32) for _ in range(NB)]
    fout = [p.sb([128, PW0], BF16) for _ in range(NB)]
    engs = ["dve", "pool", "act"]
    for i in range(NPIECE0):
        s = i % NB
        p.dma("sp", fin[s][:], wsl[:, i * PW0:(i + 1) * PW0], w=[("fin", s)])
        p.cp(engs[i % 3], fout[s][:], fin[s][:], r=[("fin", s)], w=[("fout", s)])
        p.dma("pool", wbf[:, i * PW0:(i + 1) * PW0], fout[s][:], r=[("fout", s)], is_out=True)
    return p.finish()


TE = 4608
NT = 9
EPS1 = 1e-6

def fm_chunks():
    L = []
    for j in range(4): L.append((128 * j, 128, "qk", j))
    for j in range(4): L.append((1536 + 128 * j, 128, "qk", 4 + j))
    for j in range(4): L.append((1024 + 128 * j, 128, "sg", j))
    for j in range(4): L.append((2560 + 128 * j, 128, "sg", 4 + j))
    L.append((3072, 32, "lr", 0))
    for j in range(4): L.append((3104 + 128 * j, 128, "hy", j))
    for j in range(4):
        L.append((3104 + 128 * (4 + j), 128, "hy", 4 + j))
        L.append((3104 + 128 * (8 + j), 128, "hy", 8 + j))
    for j in range(24): L.append((4640 + 128 * j, 128, "gate", j))
    return L


def build_k1():
    p = Prog()
    xT = p.dram("xT", [128, 8, TE], F32, "ExternalInput")
    win = p.dram("win", [1024, 7712], BF16, "ExternalInput")
    modsel = p.dram("modsel", [128, 48, 2], F32, "ExternalInput")
    n1g = p.dram("n1g", [128, 8], F32, "ExternalInput")
    hsw = p.dram("hsw", [128, 12, 4], F32, "ExternalInput")
    vmlr = p.dram("vmlr", [128, 2], F32, "ExternalInput")
    o_qk = p.dram("qk", [8, 128, TE], BF16, "ExternalOutput")
    o_sg = p.dram("sg", [8, 128, TE], BF16, "ExternalOutput")
    o_lr = p.dram("lrT", [32, TE], F32, "ExternalOutput")
    o_gate = p.dram("gate", [24, 128, TE], BF16, "ExternalOutput")
    o_x0 = p.dram("x0", [4, 128, TE], BF16, "ExternalOutput")
    o_z = p.dram("z", [4, 128, TE], BF16, "ExternalOutput")
    o_kv = p.dram("kv", [TE, 1536], BF16, "ExternalOutput")
    winr = win.rearrange("(k p) c -> p k c", p=128)

    hT = p.sb([128, 8, TE], BF16, "hT")
    xs = [p.sb([128, 8, 256], F32) for _ in range(2)]
    sq = p.sb([128, 8, 256], BF16)
    rstd = p.sb([128, 256], F32)
    ones = p.sb([128, 128], BF16)
    ms = p.sb([128, 48, 2], F32)
    gsb = p.sb([128, 8], F32)
    A1 = p.sb([128, 8, 2], F32)
    hs = p.sb([128, 12, 4], F32)
    vm = p.sb([128, 2], F32)
    wr = [p.sb([128, 8, 512], BF16) for _ in range(2)]
    stg = [p.sb([128, 512], BF16) for _ in range(4)]
    stg32 = [p.sb([32, 512], F32) for _ in range(2)]
    PB_ = [p.sb([128, 4612], F32) for _ in range(2)]
    cu = [p.sb([128, 512], F32) for _ in range(4)]
    zst = [p.sb([128, 512], BF16) for _ in range(2)]
    kvst = [p.sb([128, 768], BF16) for _ in range(2)]
    psA = [p.ps([128, 512], F32) for _ in range(4)]
    psB = [p.ps([128, 512], F32) for _ in range(2)]
    psS = p.ps([128, 256], F32)

    p.memset("pool", ones[:], 1.0, w=["ones"])
    epsb = p.sb([128, 1], F32)
    p.memset("pool", epsb[:], EPS1, w=["epsb"])
    p.dma("sp", ms[:], modsel, w=["ms"])
    p.dma("sp", gsb[:], n1g, w=["g"])
    p.dma("sp", hs[:], hsw, w=["hs"])
    p.dma("sp", vm[:], vmlr, w=["vm"])
    for i in range(2):
        p.memset("pool", PB_[i][:], 0.0, w=[("P", i)])
    p.ts("dve", A1[:], ms[:, 8:16, :], 1.0, None, ALU.add, r=["ms"], w=["A1"])
    for r_ in range(2):
        p.tt("dve", A1[:, :, r_], A1[:, :, r_], gsb[:], ALU.mult, r=["A1", "g"], w=["A1"])

    for hf in range(18):
        s = hf % 2
        c0 = hf * 256
        r_ = 1 if hf == 17 else 0
        p.dma("sp", xs[s][:], xT[:, :, c0:c0 + 256], w=[("x", s)])
        p.actf(sq[:], xs[s][:], AF.Square, r=[("x", s)], w=["sq"])
        for k in range(8):
            p.mm(psS[:], ones[:], sq[:, k, :], start=(k == 0), stop=(k == 7), r=["ones", "sq"], w=["psS"])
        p.actf(rstd[:], psS[:], AF.Sqrt, r=["psS", "epsb"], w=["rstd"], scale=1.0 / 1024.0, bias=epsb[:])
        p.add("dve", lambda e: e.reciprocal(rstd[:], rstd[:]), r=["rstd"], w=["rstd"])
        p.tt("dve", xs[s][:], xs[s][:], rstd[:].unsqueeze(1).broadcast_to([128, 8, 256]), ALU.mult,
             r=[("x", s), "rstd"], w=[("x", s)])
        for k in range(8):
            eng = "act" if k % 2 == 0 else "dve"
            if eng == "act":
                p.actf(hT[:, k, c0:c0 + 256], xs[s][:, k, :], AF.Identity, r=[("x", s), "A1", "ms"],
                       w=[("hT", hf)], scale=A1[:, k, r_:r_ + 1], bias=ms[:, k, r_:r_ + 1])
            else:
                p.ts("dve", hT[:, k, c0:c0 + 256], xs[s][:, k, :], A1[:, k, r_:r_ + 1], ms[:, k, r_:r_ + 1],
                     ALU.mult, ALU.add, r=[("x", s), "A1", "ms"], w=[("hT", hf)])
        if hf == 0:
            p.ts("dve", hT[:, :, 0:128], hT[:, :, 0:128], vm[:, 0:1], None, ALU.mult, r=["vm", ("hT", 0)], w=[("hT", 0)])
        if hf == 16:
            p.ts("dve", hT[:, :, 4224:4352], hT[:, :, 4224:4352], vm[:, 1:2], None, ALU.mult,
                 r=["vm", ("hT", 16)], w=[("hT", 16)])

    chunks = fm_chunks()
    groups = []
    for ch in chunks:
        if groups and groups[-1][0] + groups[-1][1] == ch[0] and groups[-1][1] + ch[1] <= 512 and ch[2] != "hy" and groups[-1][2][0][2] != "hy":
            groups[-1][1] += ch[1]
            groups[-1][2].append(ch)
        else:
            groups.append([ch[0], ch[1], [ch]])
    nst = 0
    npa = 0
    ncu = 0
    nz = 0
    evq = 0
    for gi, (g0, gw, chs) in enumerate(groups):
        ws = gi % 2
        p.dma("sp", wr[ws][:, :, 0:gw], winr[:, :, g0:g0 + gw], w=[("w", ws)])
        for (c0, M, kind, idx) in chs:
            off = c0 - g0
            for t in range(NT):
                ps = psA[npa % 4]; pst = ("psA", npa % 4); npa += 1
                for k in range(8):
                    p.mm(ps[0:M, :], wr[ws][:, k, off:off + M], hT[:, k, t * 512:(t + 1) * 512],
                         start=(k == 0), stop=(k == 7), r=[("w", ws), ("hT", 2 * t), ("hT", 2 * t + 1)], w=[pst])
                cols = slice(t * 512, (t + 1) * 512)
                if kind in ("qk", "sg", "gate"):
                    st = stg[nst % 4]; stt_ = ("stg", nst % 4); nst += 1
                    if kind == "qk":
                        eng = "dve" if evq % 2 == 0 else "act"; evq += 1
                        p.cp(eng, st[:], ps[:], r=[pst], w=[stt_])
                        dst = o_qk[idx, :, cols]
                    elif kind == "sg":
                        p.actf(st[:], ps[:], AF.Silu, r=[pst], w=[stt_])
                        dst = o_sg[idx, :, cols]
                    else:
                        p.actf(st[:], ps[:], AF.Sigmoid, r=[pst], w=[stt_])
                        dst = o_gate[idx, :, cols]
                    p.dma("pool", dst, st[:], r=[stt_], is_out=True)
                elif kind == "lr":
                    st = stg32[t % 2]; stt_ = ("stg32", t % 2)
                    p.cp("dve", st[:], ps[0:32, :], r=[pst], w=[stt_])
                    p.dma("pool", o_lr[:, cols], st[:], r=[stt_], is_out=True)
                else:
                    bi = 0 if idx < 8 else 1
                    buf = PB_[bi]
                    if t < 8:
                        p.cp("dve", buf[:, 1 + t * 512: 1 + (t + 1) * 512], ps[:], r=[pst], w=[("P", bi)])
                    else:
                        p.cp("dve", buf[:, 4097:4353], ps[:, 0:256], r=[pst], w=[("P", bi)])
                        p.cp("dve", buf[:, 4355:4611], ps[:, 256:512], r=[pst], w=[("P", bi)])
            if kind == "hy" and (idx < 4 or idx >= 8):
                pieces = [(1 + 512 * i, 512, 512 * i) for i in range(8)] + [(4097, 256, 4096), (4355, 256, 4352)]
                for (b0, n, tc) in pieces:
                    def conv(bi, j, eng):
                        nonlocal ncu
                        u = cu[ncu % 4]; ut = ("cu", ncu % 4); ncu += 1
                        buf = PB_[bi]
                        p.ts(eng, u[:, 0:n], buf[:, b0:b0 + n], hs[:, j, 1:2], hs[:, j, 3:4], ALU.mult, ALU.add,
                             r=[("P", bi), "hs"], w=[ut])
                        p.stt(eng, u[:, 0:n], buf[:, b0 - 1:b0 - 1 + n], hs[:, j, 0:1], u[:, 0:n], ALU.mult, ALU.add,
                              r=[("P", bi), "hs", ut], w=[ut])
                        p.stt(eng, u[:, 0:n], buf[:, b0 + 1:b0 + 1 + n], hs[:, j, 2:3], u[:, 0:n], ALU.mult, ALU.add,
                              r=[("P", bi), "hs", ut], w=[ut])
                        return u, ut
                    zs = zst[nz % 2]; zt = ("zst", nz % 2); nz += 1
                    if idx < 4:
                        u, ut = conv(0, idx, "dve")
                        p.cp("act", zs[:, 0:n], u[:, 0:n], r=[ut], w=[zt])
                        p.dma("pool", o_x0[idx, :, tc:tc + n], zs[:, 0:n], r=[zt], is_out=True)
                    else:
                        ua, uat = conv(0, idx - 4, "dve")
                        ub, ubt = conv(1, idx, "dve")
                        p.tt("dve", zs[:, 0:n], ua[:, 0:n], ub[:, 0:n], ALU.mult, r=[uat, ubt], w=[zt])
                        p.dma("pool", o_z[idx - 8, :, tc:tc + n], zs[:, 0:n], r=[zt], is_out=True)

    wkv = [PB_[i][:, 0:3072].bitcast(BF16).rearrange("p (k c) -> p k c", k=8) for i in range(2)]
    for gi, c0 in enumerate((256, 1792)):
        p.dma("sp", wkv[gi], winr[:, :, c0:c0 + 768], w=[("P", gi)])
    nkv = 0
    for tb in range(TE // 128):
        for gi in range(2):
            pa = psB[0]; pb = psB[1]
            for k in range(8):
                p.mm(pa[:], hT[:, k, tb * 128:(tb + 1) * 128], wkv[gi][:, k, 0:512], start=(k == 0), stop=(k == 7),
                     r=[("P", gi), ("hT", tb // 2)], w=["psB0"])
            for k in range(8):
                p.mm(pb[:, 0:256], hT[:, k, tb * 128:(tb + 1) * 128], wkv[gi][:, k, 512:768], start=(k == 0), stop=(k == 7),
                     r=[("P", gi), ("hT", tb // 2)], w=["psB1"])
            st = kvst[nkv % 2]; stt_ = ("kvst", nkv % 2); nkv += 1
            p.cp("dve", st[:, 0:512], pa[:], r=["psB0"], w=[stt_])
            p.cp("act", st[:, 512:768], pb[:, 0:256], r=["psB1"], w=[stt_])
            p.dma("pool", o_kv[tb * 128:(tb + 1) * 128, gi * 768:(gi + 1) * 768], st[:], r=[stt_], is_out=True)
    return p.finish()


I32 = mybir.dt.int32
NG = 32
TT = 256 + 512 * NG
NCH = 2 + 4 * NG
HEADS = [0, 1]
TICK_DIV = 1
STAGE = 9
LL = 16384
HEPS = 1e-5
NLAG = 17
NPIECE = 15
TWO_PI = 2.0 * math.pi


def groups_fwd():
    return [(0, 2)] + [(2 + 4 * i, 4) for i in range(NG)]


def build_k2(do_scan=True, do_hy=True, do_ctx_out=True):
    p = Prog()
    qT = p.dram("qT", [2, 64, TT], BF16, "ExternalInput")
    kT = p.dram("kT", [2, 64, TT], BF16, "ExternalInput")
    ktok = p.dram("ktok", [2, TT, 64], BF16, "ExternalInput")
    vtok = p.dram("vtok", [2, TT, 128], BF16, "ExternalInput")
    lrT = p.dram("lrT", [34, TT], F32, "ExternalInput")
    wa = p.dram("wa", [34, 128], F32, "ExternalInput")
    laR = p.dram("laR", [128, 2, 64], F32, "ExternalInput")
    tri = p.dram("tri", [128, 4, 128], F32, "ExternalInput")
    msk = p.dram("msk", [128, 2, 128], F32, "ExternalInput")
    o_on = p.dram("onT", [2, 128, TT], BF16, "ExternalOutput")
    zf = p.dram("zf", [2, 33, LL], F32, "ExternalInput")
    zfc = p.dram("zfc", [2, 33, 256], F32, "ExternalInput")
    win = p.dram("win", [2, 64, LL], F32, "ExternalInput")
    winc = p.dram("winc", [2, 64, 256], F32, "ExternalInput")
    hw1 = p.dram("hw1", [33, 64], F32, "ExternalInput")
    hb = p.dram("hb", [64, 4], F32, "ExternalInput")
    hw2 = p.dram("hw2", [64, 2, 64], F32, "ExternalInput")
    hw3 = p.dram("hw3", [64, 2, 64], F32, "ExternalInput")
    zrev = p.dram("zrev", [128, 64, 256], BF16, "ExternalInput")
    znat = p.dram("znat", [128, 64, 256], BF16, "ExternalInput")
    zrevc = p.dram("zrevc", [128, 64, 4], BF16, "ExternalInput")
    znatc = p.dram("znatc", [128, 64, 4], BF16, "ExternalInput")
    hbias = p.dram("hbias", [128, 64], F32, "ExternalInput")
    o_y = p.dram("Y", [64, 128, 256], BF16, "ExternalOutput")
    o_yc = p.dram("Yc", [64, 128, 4], BF16, "ExternalOutput")
    KF = p.nc.dram_tensor("KF", [64, 32768], BF16, kind="Internal")
    KFc = p.nc.dram_tensor("KFc", [64, 512], BF16, kind="Internal")

    B = [p.ps([128, 512], F32, f"B{i}") for i in range(8)]

    def bt(i, *h):
        return [("B", i)]

    tri_sb = p.sb([128, 4, 128], F32)
    msk_sb = p.sb([128, 2, 128], F32)
    p.dma("sp", tri_sb[:], tri, w=["tri"])
    p.dma("sp", msk_sb[:], msk, w=["msk"])
    negc = p.sb([128, 1], F32)
    p.memset("pool", negc[:], -1.0 / 16.0, w=["negc"])
    one_c = p.sb([128, 1], F32)
    p.memset("pool", one_c[:], 1.0, w=["one_c"])
    heps_c = p.sb([128, 1], F32)
    p.memset("pool", heps_c[:], HEPS, w=["heps"])
    onesN = p.sb([128, 128], F32)
    p.memset("pool", onesN[:], 1.0 / 128.0, w=["onesN"])

    def emit_scans(tick):
        wa_sb = p.sb([34, 128], F32)
        p.dma("sp", wa_sb[:], wa, w=["wa"])
        laR_sb = p.sb([128, 1, 2, 64], F32)
        p.dma("sp", laR_sb[:, 0, :, :], laR, w=["laR"])
        Gst = p.sb([64, NCH, 128], BF16)
        lr_sb = [p.sb([34, 512], F32) for _ in range(2)]
        la_sb = p.sb([128, 4, 2, 64], F32)
        tex = p.sb([128, 4, 2, 64], F32)
        edte = p.sb([128, 4, 2, 64], F32)
        kdec = p.sb([128, 4, 2, 64], BF16)
        A_sb = p.sb([64, 4, 2], F32)
        q_sb = [p.sb([64, 512], BF16) for _ in range(2)]
        k_sb = [p.sb([64, 512], BF16) for _ in range(2)]
        kt_sb = [p.sb([128, 4, 64], BF16) for _ in range(2)]
        vt_sb = [p.sb([128, 4, 128], BF16) for _ in range(2)]
        E_sb = p.sb([64, 2, 2, 512], F32)
        qx = p.sb([64, 2, 512], BF16)
        kx = p.sb([64, 2, 512], BF16)
        tS = p.sb([128, 512], F32)
        tS2 = p.sb([128, 512], F32)
        P_sb = p.sb([128, 512], BF16)
        o32 = p.sb([128, 512], F32)
        dsb = p.sb([128, 512], F32)
        sqs = p.sb([128, 512], F32)
        rs = p.sb([128, 512], F32)
        on_sb = [p.sb([128, 512], BF16) for _ in range(2)]
        Hst = p.sb([64, 4, 128], BF16)
        Hs = p.sb([64, 128], F32)
        Gs = p.sb([64, 128], F32)
        cnt = {"ld": 0, "on": 0, "U": 0}

        def decay(g, c0, n, dirs, need_cum, slot):
            W = n * 128
            if g == 1:
                p.dma("act", lr_sb[slot][:, 0:W], lrT[:, c0 * 128:c0 * 128 + W], w=[("lr", slot)])
                for c in range(n):
                    p.mm(B[0][:, c * 128:(c + 1) * 128], lr_sb[slot][:, c * 128:(c + 1) * 128], wa_sb[:], r=[("lr", slot), "wa"], w=bt(0))
                tick()
                psx = B[0][:, 0:n * 128].rearrange("p (c d e) -> p c d e", c=n, d=2)
                for d in dirs:
                    p.actf(tex[:, 0:n, d, :], psx[:, :, d, :], AF.Exp, r=bt(0), w=["tex"], scale=-1.0)
                    p.actf(la_sb[:, 0:n, d, :], tex[:, 0:n, d, :], AF.Ln, r=["tex", "one_c"], w=["la"], bias=one_c[:])
                la = la_sb
                lat = "la"
                ncmp = n
            else:
                la = laR_sb
                lat = "laR"
                ncmp = 1
                if ("ret_done", tuple(dirs), need_cum) in cnt:
                    pass
            key = ("retc", need_cum)
            if g == 0 and key in cnt:
                return cnt[key](n)
            for c in range(ncmp):
                for d in dirs:
                    p.mm(B[0][:, (c * 2 + d) * 64:(c * 2 + d + 1) * 64], tri_sb[:, 2 + d, :], la[:, c, d, :],
                         r=["tri", lat], w=bt(0))
                    if not need_cum:
                        p.mm(B[0][0:64, (c * 2) * 64:(c * 2) * 64 + 1], la[:, c, d, :], negc[:], r=[lat, "negc"], w=bt(0))
            tick()
            psd = B[0][:, 0:ncmp * 128].rearrange("p (c d e) -> p c d e", c=ncmp, d=2)
            if g == 0:
                ed = p.sb([128, 1, 2, 64], F32)
                As = p.sb([64, 1, 2], F32)
                edt, Ast = ("edR", need_cum), ("AR", need_cum)
            else:
                ed, As, edt, Ast = edte, A_sb, "edte", "A"
            for d in dirs:
                p.actf(ed[:, 0:ncmp, d, :], psd[:, :, d, :], AF.Exp, r=bt(0), w=[edt])
                if not need_cum:
                    p.actf(As[:, 0:ncmp, d], psd[0:64, :, 0, 0], AF.Exp, r=bt(0), w=[Ast])
            res = {"ed": ed, "edt": edt, "A": As, "At": Ast}
            if need_cum:
                if g == 0:
                    Et_ = p.sb([64, 2, 2, 128], F32)
                    Ett = "ER"
                else:
                    Et_, Ett = E_sb, "E"
                for d in (0, 1):
                    for c in range(ncmp):
                        p.mm(B[2 + d][0:64, c * 128:(c + 1) * 128], la[:, c, d, :], tri_sb[:, d, :], r=[lat, "tri"], w=bt(2 + d))
                    tick()
                    p.actf(Et_[:, d, 0, 0:ncmp * 128], B[2 + d][0:64, 0:ncmp * 128], AF.Exp, r=bt(2 + d), w=[Ett])
                    p.actf(Et_[:, d, 1, 0:ncmp * 128], B[2 + d][0:64, 0:ncmp * 128], AF.Exp, r=bt(2 + d), w=[Ett], scale=-1.0)
                res["E"] = Et_
                res["Et"] = Ett
            if g == 0:
                def mk(nn, res=res):
                    out = {"edt": res["edt"], "At": (res["Et"] if "E" in res else res["At"])}
                    out["ed"] = lambda d: res["ed"][:, 0:1, d, :].broadcast_to([128, nn, 64])
                    if "E" in res:
                        out["A"] = lambda c, d: res["E"][:, d, 0, (127 if d == 0 else 0):(128 if d == 0 else 1)]
                    else:
                        out["A"] = lambda c, d: res["A"][:, 0, d:d + 1]
                    if "E" in res:
                        out["Et"] = res["Et"]
                        out["E"] = lambda d, i: res["E"][:, d, i, :].unsqueeze(1).broadcast_to([64, nn, 128])
                    return out
                cnt[key] = mk
                return mk(n)
            out = {"edt": edt, "At": Ast}
            out["ed"] = lambda d: ed[:, 0:n, d, :]
            out["A"] = lambda c, d: As[:, c, d:d + 1]
            if need_cum:
                out["A"] = lambda c, d: E_sb[:, d, 0, c * 128 + (127 if d == 0 else 0):c * 128 + (128 if d == 0 else 1)]
                out["At"] = res["Et"]
                out["Et"] = res["Et"]
                out["E"] = lambda d, i: E_sb[:, d, i, 0:n * 128].rearrange("p (c t) -> p c t", c=n)
            return out

        def load_kv(g, c0, n, slot, with_qk):
            W = n * 128
            p.dma("act", kt_sb[slot][:, 0:n, :], ktok[g, c0 * 128:c0 * 128 + W, :].rearrange("(c p) d -> p c d", p=128), w=[("kt", slot)])
            p.dma("act", vt_sb[slot][:, 0:n, :], vtok[g, c0 * 128:c0 * 128 + W, :].rearrange("(c p) d -> p c d", p=128), w=[("vt", slot)])
            if with_qk:
                p.dma("act", q_sb[slot][:, 0:W], qT[g, :, c0 * 128:c0 * 128 + W], w=[("q", slot)])
                p.dma("act", k_sb[slot][:, 0:W], kT[g, :, c0 * 128:c0 * 128 + W], w=[("k", slot)])

        for g in HEADS:
            p.memset("pool", Gs[:], 0.0, w=["G"])
            grpsA = [(0, 2)] + [(2 + 4 * i, 4) for i in reversed(range(NG))]
            for (c0, n) in grpsA:
                slot = cnt["ld"] % 2
                cnt["ld"] += 1
                load_kv(g, c0, n, slot, False)
                dk = decay(g, c0, n, (1,), False, slot)
                p.tt("dve", kdec[:, 0:n, 1, :], kt_sb[slot][:, 0:n, :], dk["ed"](1), ALU.mult, r=[("kt", slot), dk["edt"]], w=["kdec"])
                for c in reversed(range(n)):
                    j = c0 + c
                    p.cp("pool", Gst[:, j, :], Gs[:], r=["G"], w=[("Gst", j)])
                    us = cnt["U"] % 2
                    cnt["U"] += 1
                    ub = 1
                    p.mm(B[ub][0:64, 0:128], kdec[:, c, 1, :], vt_sb[slot][:, c, :], r=["kdec", ("vt", slot)], w=bt(ub))
                    tick()
                    p.stt("dve", Gs[:], Gs[:], dk["A"](c, 1), B[ub][0:64, 0:128], ALU.mult, ALU.add,
                          r=["G", dk["At"]] + bt(ub), w=["G"])
            p.memset("pool", Hs[:], 0.0, w=["H"])
            for (c0, n) in (groups_fwd() if STAGE >= 2 else []):
                W = n * 128
                slot = cnt["ld"] % 2
                cnt["ld"] += 1
                load_kv(g, c0, n, slot, True)
                dk = decay(g, c0, n, (0, 1), True, slot)
                q3 = q_sb[slot][:, 0:W].rearrange("p (c t) -> p c t", c=n)
                k3 = k_sb[slot][:, 0:W].rearrange("p (c t) -> p c t", c=n)
                for d in (0, 1):
                    p.stt("dve", qx[:, d, 0:W].rearrange("p (c t) -> p c t", c=n), q3, 0.125, dk["E"](d, 0), ALU.mult, ALU.mult,
                          r=[("q", slot), dk["Et"]], w=["qx"])
                    p.tt("dve", kx[:, d, 0:W].rearrange("p (c t) -> p c t", c=n), k3, dk["E"](d, 1), ALU.mult,
                         r=[("k", slot), dk["Et"]], w=["kx"])
                p.tt("dve", kdec[:, 0:n, 0, :], kt_sb[slot][:, 0:n, :], dk["ed"](0), ALU.mult, r=[("kt", slot), dk["edt"]], w=["kdec"])
                for d in (0, 1):
                    for c in range(n):
                        p.mm(B[4 + d][:, c * 128:(c + 1) * 128], kx[:, d, c * 128:(c + 1) * 128], qx[:, d, c * 128:(c + 1) * 128],
                             r=["kx", "qx"], w=bt(4 + d))
                tick()
                m0 = msk_sb[:, 0, :].unsqueeze(1).broadcast_to([128, n, 128])
                m1 = msk_sb[:, 1, :].unsqueeze(1).broadcast_to([128, n, 128])
                v3 = lambda t: t[:, 0:W].rearrange("p (c t) -> p c t", c=n)
                p.tt("dve", v3(tS), v3(B[4]), m0, ALU.mult, r=bt(4) + ["msk"], w=["tS"])
                p.tt("dve", v3(tS2), v3(B[5]), m1, ALU.mult, r=bt(5) + ["msk"], w=["tS2"])
                p.tt("pool", P_sb[:, 0:W], tS[:, 0:W], tS2[:, 0:W], ALU.add, r=["tS", "tS2"], w=["P"])
                for c in range(n):
                    j = c0 + c
                    p.cp("pool", Hst[:, c, :], Hs[:], r=["H"], w=["Hst"])
                    us = cnt["U"] % 2
                    cnt["U"] += 1
                    ub = 1
                    p.mm(B[ub][0:64, 0:128], kdec[:, c, 0, :], vt_sb[slot][:, c, :], r=["kdec", ("vt", slot)], w=bt(ub))
                    p.stt("dve", Hs[:], Hs[:], dk["A"](c, 0), B[ub][0:64, 0:128], ALU.mult, ALU.add,
                          r=["H", dk["At"]] + bt(ub), w=["H"])
                    oc = B[6][:, c * 128:(c + 1) * 128]
                    p.mm(oc, vt_sb[slot][:, c, :], P_sb[:, c * 128:(c + 1) * 128], start=True, stop=False, r=[("vt", slot), "P"], w=bt(6))
                    p.mm(oc, Hst[:, c, :], qx[:, 0, c * 128:(c + 1) * 128], start=False, stop=False, r=["Hst", "qx"], w=bt(6))
                    p.mm(oc, Gst[:, j, :], qx[:, 1, c * 128:(c + 1) * 128], start=False, stop=True, r=[("Gst", j), "qx"], w=bt(6))
                    tick()
                p.cp("act", o32[:, 0:W], B[6][:, 0:W], r=bt(6), w=["o32"])
                p.mm(B[2][:, 0:W], onesN[:], o32[:, 0:W], r=["onesN", "o32"], w=bt(2))
                tick()
                p.tt("dve", dsb[:, 0:W], o32[:, 0:W], B[2][:, 0:W], ALU.subtract, r=["o32"] + bt(2), w=["d"])
                p.actf(sqs[:, 0:W], dsb[:, 0:W], AF.Square, r=["d"], w=["sq"])
                p.mm(B[3][:, 0:W], onesN[:], sqs[:, 0:W], r=["onesN", "sq"], w=bt(3))
                tick()
                p.actf(rs[:, 0:W], B[3][:, 0:W], AF.Sqrt, r=bt(3) + ["heps"], w=["rs"], bias=heps_c[:])
                p.add("dve", lambda e, W=W: e.reciprocal(rs[:, 0:W], rs[:, 0:W]), r=["rs"], w=["rs"])
                os_ = cnt["on"] % 2
                cnt["on"] += 1
                p.tt("dve", on_sb[os_][:, 0:W], dsb[:, 0:W], rs[:, 0:W], ALU.mult, r=["d", "rs"], w=[("on", os_)])
                p.dma("pool", o_on[g, :, c0 * 128:c0 * 128 + W], on_sb[os_][:, 0:W], r=[("on", os_)], is_out=True)

    if do_hy:
        w1_sb = p.sb([33, 64], F32)
        hb_sb = p.sb([64, 4], F32)
        w2_sb = p.sb([64, 2, 64], F32)
        w3_sb = p.sb([64, 2, 64], F32)
        fb_sb = p.sb([64, 3], F32)
        p.dma("sp", w1_sb[:], hw1, w=["w1"])
        p.dma("sp", hb_sb[:], hb, w=["hb"])
        p.dma("sp", w2_sb[:], hw2, w=["w2"])
        p.dma("sp", w3_sb[:], hw3, w=["w3"])
        p.ts("dve", fb_sb[:], hb_sb[:, 0:3], hb_sb[:, 3:4], None, ALU.mult, r=["hb"], w=["fb"])
        zt_sb = [p.sb([33, 512], F32) for _ in range(2)]
        wt_sb = [p.sb([64, 512], F32) for _ in range(2)]
        a_sb = p.sb([64, 512], F32)
        ki_sb = p.sb([64, 512], I32)
        r_sb = p.sb([64, 512], F32)
        h_sb = [p.sb([64, 512], F32) for _ in range(2)]
        fo_sb = [p.sb([64, 512], BF16) for _ in range(2)]
        mc = {"t": 0}
        kf_toks = {}

        def mlp(zsrc, wsrc, L, kf_t, kf_len):
            ntile = max(1, L // 512)
            n = min(L, 512)
            for run_ in (1, 0):
                for t in range(ntile):
                    s = mc["t"] % 2
                    mc["t"] += 1
                    p.dma("sp", zt_sb[s][:, 0:n], zsrc[run_, :, t * n:(t + 1) * n], w=[("zt", s)])
                    p.dma("sp", wt_sb[s][:, 0:n], wsrc[run_, :, t * n:(t + 1) * n], w=[("wt", s)])
                    bk = B[s]
                    cur_l, cur_r, cur_t = w1_sb[:], zt_sb[s][:, 0:n], ["w1", ("zt", s)]
                    for layer in range(3):
                        p.mm(bk[0:64, 0:n], cur_l, cur_r, r=cur_t, w=bt(s))
                        p.ts("dve", a_sb[:, 0:n], bk[0:64, 0:n], hb_sb[:, 3:4], fb_sb[:, layer:layer + 1], ALU.mult, ALU.add,
                             r=bt(s) + ["hb", "fb"], w=["a"])
                        p.ts("dve", ki_sb[:, 0:n], a_sb[:, 0:n], 1.0 / TWO_PI, None, ALU.mult, r=["a"], w=["ki"])
                        p.stt("dve", r_sb[:, 0:n], ki_sb[:, 0:n], -TWO_PI, a_sb[:, 0:n], ALU.mult, ALU.add, r=["ki", "a"], w=["r"])
                        hh = h_sb[layer % 2]
                        p.actf(hh[:, 0:n], r_sb[:, 0:n], AF.Sin, r=["r"], w=[("h", layer % 2)])
                        if layer < 2:
                            cur_l, cur_r, cur_t = w2_sb[:, layer, :], hh[:, 0:n], ["w2", ("h", layer % 2)]
                        else:
                            cur_l, cur_r, cur_t = w3_sb[:, run_, :], hh[:, 0:n], ["w3", ("h", layer % 2)]
                    p.mm(bk[0:64, 0:n], cur_l, cur_r, r=cur_t, w=bt(s))
                    p.tt("dve", fo_sb[s][:, 0:n], bk[0:64, 0:n], wt_sb[s][:, 0:n], ALU.mult, r=bt(s) + [("wt", s)], w=[("fo", s)])
                    if run_ == 1:
                        nout = n - 1 if t == ntile - 1 else n
                        dst = AP(kf_t, t * n, [[kf_len, 64], [1, nout]])
                    else:
                        nout = n
                        dst = AP(kf_t, L - 1 + t * n, [[kf_len, 64], [1, nout]])
                    p.dma("pool", dst, fo_sb[s][:, 0:nout], r=[("fo", s)], w=[("KF", kf_len, run_, t)])
                    kf_toks.setdefault(kf_len, []).append(("KF", kf_len, run_, t))

        mlp(zf, win, LL, KF, 32768)
        mlp(zfc, winc, 256, KFc, 512)

        zr_sb = p.sb([128, 64, 256], BF16)
        zn_sb = p.sb([128, 64, 256], BF16)
        zrc_sb = p.sb([128, 64, 4], BF16)
        znc_sb = p.sb([128, 64, 4], BF16)
        hbias_sb = p.sb([128, 64], F32)
        p.dma("sp", zr_sb[:], zrev, w=["zr"])
        p.dma("sp", zn_sb[:], znat, w=["zn"])
        p.dma("sp", zrc_sb[:], zrevc, w=["zrc"])
        p.dma("sp", znc_sb[:], znatc, w=["znc"])
        p.dma("sp", hbias_sb[:], hbias, w=["hbias"])
        NS = 6
        strip = [p.sb([128, NLAG * 128], BF16) for _ in range(NS)]
        yst = [p.sb([128, 256], BF16) for _ in range(4)]
        sc = {"s": 0, "y": 0}
        order = [7] + [i for i in range(NPIECE) if i != 7]

        def conv_gen():
          for c in range(64):
              yb = 0
              psY = B[7][:, 0:256]
              first = True
              nmm = 0
              for pc in order:
                  s = sc["s"] % NS
                  sc["s"] += 1
                  n0 = pc * NLAG * 128
                  src = AP(KF, c * 32768 + n0, [[1, 128], [1, NLAG * 128]])
                  q = "sp"
                  p.dma(q, strip[s][:], src, r=kf_toks[32768], w=[("strip", s)])
                  lis = list(range(pc * NLAG, (pc + 1) * NLAG))
                  if pc == 7:
                      lis = [127] + [x for x in lis if x != 127]
                  for li in lis:
                      d = li - 127
                      I0, I1 = max(0, d), min(128, 128 + d)
                      J0, J1 = I0 - d, I1 - d
                      nmm += 1
                      p.mm(psY[:, 2 * I0:2 * I1], strip[s][:, (li - pc * NLAG) * 128:(li - pc * NLAG + 1) * 128], zr_sb[:, c, 2 * J0:2 * J1],
                           start=first, stop=(nmm == 255), r=[("strip", s), "zr"], w=bt(7))
                      first = False
                  yield
              ys = sc["y"] % 4
              sc["y"] += 1
              p.stt("dve", yst[ys][:], zn_sb[:, c, :], hbias_sb[:, c:c + 1], B[7][:, 0:256], ALU.mult, ALU.add,
                    r=["zn", "hbias"] + bt(7), w=[("yst", ys)])
              p.dma("pool", o_y[c], yst[ys][:], r=[("yst", ys)], is_out=True)
        gen = conv_gen()
        tk = {"n": 0}

        def tick():
            tk["n"] += 1
            if TICK_DIV < 0:
                for _ in range(-TICK_DIV):
                    next(gen, None)
            elif tk["n"] % TICK_DIV == 0:
                next(gen, None)
        if do_scan:
            emit_scans(tick)
        for _ in gen:
            pass
        stc = [p.sb([128, 384], BF16) for _ in range(2)]
        ystc = [p.sb([128, 4], BF16) for _ in range(2)]
        for c in range(64):
            s = c % 2
            src = AP(KFc, c * 512, [[1, 128], [1, 384]])
            p.dma("sp", stc[s][:], src, r=kf_toks[512], w=[("stc", s)])
            psY = B[4 + s][:, 0:4]
            for idx, d in enumerate((0, -1, 1)):
                li = d + 1
                I0, I1 = max(0, d), min(2, 2 + d)
                J0, J1 = I0 - d, I1 - d
                p.mm(psY[:, 2 * I0:2 * I1], stc[s][:, li * 128:(li + 1) * 128], zrc_sb[:, c, 2 * J0:2 * J1], start=(idx == 0), stop=(idx == 2),
                     r=[("stc", s), "zrc"], w=bt(4 + s))
            p.stt("dve", ystc[s][:], znc_sb[:, c, :], hbias_sb[:, c:c + 1], B[4 + s][:, 0:4], ALU.mult, ALU.add,
                  r=["znc", "hbias"] + bt(4 + s), w=[("ystc", s)])
            p.dma("pool", o_yc[c], ystc[s][:], r=[("ystc", s)], is_out=True)
    if do_scan and not do_hy:
        emit_scans(lambda: None)
    return p.finish()


TE = 4608
EPS1 = 1e-6


def build_k3(last=False):
    p = Prog()
    xT = p.dram("xT", [128, 8, TE], F32, "ExternalInput")
    onr = p.dram("onr", [4, 128, TE], BF16, "ExternalInput")
    ong = p.dram("ong", [4, 128, TE], BF16, "ExternalInput")
    sg = p.dram("sg", [8, 128, TE], BF16, "ExternalInput")
    x0 = p.dram("x0", [4, 128, TE], BF16, "ExternalInput")
    yh = p.dram("yh", [4, 128, TE], BF16, "ExternalInput")
    gate = p.dram("gate", [24, 128, TE], BF16, "ExternalInput")
    wbr = p.dram("wbr", [3, 8, 128, 4, 128], BF16, "ExternalInput")
    wout = p.dram("wout", [8, 128, 8, 128], BF16, "ExternalInput")
    wup = p.dram("wup", [44, 128, 8, 128], BF16, "ExternalInput")
    wdn = p.dram("wdn", [8, 128, 22, 128], BF16, "ExternalInput")
    modsel = p.dram("modsel", [128, 48, 2], F32, "ExternalInput")
    n2g = p.dram("n2g", [128, 8], F32, "ExternalInput")
    fconv = p.dram("fconv", [128, 22, 10], F32, "ExternalInput")
    fing = p.dram("fing", [128, 8], F32, "ExternalInput")
    vmlr = p.dram("vmlr", [128, 2], F32, "ExternalInput")
    NOUT = 4096 if last else 4352
    o_x = p.dram("xo", [128, 8, NOUT], F32, "ExternalOutput")

    ones = p.sb([128, 128], BF16)
    p.memset("pool", ones[:], 1.0, w=["ones"])
    epsb = p.sb([128, 1], F32)
    p.memset("pool", epsb[:], EPS1, w=["epsb"])
    ms = p.sb([128, 48, 2], F32)
    g2n = p.sb([128, 8], F32)
    fc_sb = p.sb([128, 22, 10], F32)
    fg_sb = p.sb([128, 8], F32)
    vm = p.sb([128, 2], F32)
    A2 = p.sb([128, 8, 2], F32)
    p.dma("sp", ms[:], modsel, w=["ms"])
    p.dma("sp", g2n[:], n2g, w=["g2n"])
    p.dma("sp", fc_sb[:], fconv, w=["fc"])
    p.dma("sp", fg_sb[:], fing, w=["fg"])
    p.dma("sp", vm[:], vmlr, w=["vm"])
    p.ts("dve", A2[:], ms[:, 32:40, :], 1.0, None, ALU.add, r=["ms"], w=["A2"])
    for r_ in range(2):
        p.tt("dve", A2[:, :, r_], A2[:, :, r_], g2n[:], ALU.mult, r=["A2", "g2n"], w=["A2"])

    NX1 = 3
    x1T = [p.sb([128, 8, 512], F32, f"x1T{i}") for i in range(NX1)]
    h2T = [p.sb([128, 8, 512], BF16, f"h2T{i}") for i in range(NX1)]
    inA = [p.sb([128, 512], BF16) for _ in range(2)]
    inB = [p.sb([128, 512], BF16) for _ in range(2)]
    bT = p.sb([128, 12, 512], BF16)
    gt = [p.sb([128, 3, 512], BF16) for _ in range(2)]
    xin = [p.sb([128, 512], F32) for _ in range(2)]
    mixT = p.sb([128, 8, 512], BF16)
    t1 = p.sb([128, 512], F32)
    t2 = p.sb([128, 512], F32)
    t3 = p.sb([128, 512], F32)
    sq = [p.sb([128, 512], BF16) for _ in range(2)]
    rstd = p.sb([128, 512], F32)
    rstd2 = p.sb([128, 512], F32) if last else None
    xn = [p.sb([128, 512], F32) for _ in range(2)]
    wbr_sb = [p.sb([128, 4, 128], BF16) for _ in range(2)]
    wout_sb = [p.sb([128, 8, 128], BF16) for _ in range(2)]
    wup_sb = [p.sb([128, 8, 128], BF16) for _ in range(4)]
    wdn_sb = [p.sb([128, 22, 128], BF16) for _ in range(2)]
    a_sb = [p.sb([128, 640], F32) for _ in range(2)]
    acc = [p.sb([128, 512], F32) for _ in range(2)]
    gl = [p.sb([128, 512], F32) for _ in range(2)]
    uT = p.sb([128, 22, 512], BF16)
    xo = p.sb([128, 8, 512], F32) if last else None
    xor = [p.sb([128, 512], F32) for _ in range(2)] if not last else None
    B = [p.ps([128, 512], F32, f"B{i}") for i in range(8)]
    cn = {"in": 0, "gt": 0, "xin": 0, "wbr": 0, "wout": 0, "wup": 0, "wdn": 0, "a": 0, "ps": 0, "p1": 0}

    def bk(i):
        return ("B", i)

    def P1(t):
        c0 = t * 512
        cols = slice(c0, c0 + 512)
        slot = t % NX1
        srcs = [(onr, j, sg, j) for j in range(4)] + [(ong, j, sg, 4 + j) for j in range(4)] + [(x0, j, yh, j) for j in range(4)]
        for bi, (sa, ja, sb_, jb) in enumerate(srcs):
            s = cn["in"] % 2
            cn["in"] += 1
            p.dma("sp", inA[s][:], sa[ja, :, cols], w=[("inA", s)])
            p.dma("sp", inB[s][:], sb_[jb, :, cols], w=[("inB", s)])
            p.tt("pool", bT[:, bi, :], inA[s][:], inB[s][:], ALU.mult, r=[("inA", s), ("inB", s)], w=[("bT", bi)])
            if bi % 4 == 3:
                yield
        for oc in range(8):
            gs = cn["gt"] % 2
            cn["gt"] += 1
            for br in range(3):
                p.dma("sp", gt[gs][:, br, :], gate[br * 8 + oc, :, cols], w=[("gt", gs)])
            for br in range(3):
                ws = cn["wbr"] % 2
                cn["wbr"] += 1
                p.dma("sp", wbr_sb[ws][:], wbr[br, oc], w=[("wbr", ws)])
                pb = 6 + cn["p1"] % 2
                cn["p1"] += 1
                for kc in range(4):
                    p.mm(B[pb][:], wbr_sb[ws][:, kc, :], bT[:, br * 4 + kc, :], start=(kc == 0), stop=(kc == 3),
                         r=[("wbr", ws), ("bT", br * 4 + kc)], w=[bk(pb)])
                if br == 0:
                    p.tt("dve", t1[:], B[pb][:], gt[gs][:, 0, :], ALU.mult, r=[bk(pb), ("gt", gs)], w=["t1"])
                elif br == 1:
                    p.tt("dve", t2[:], B[pb][:], gt[gs][:, 1, :], ALU.mult, r=[bk(pb), ("gt", gs)], w=["t2"])
                    p.tt("pool", t1[:], t1[:], t2[:], ALU.add, r=["t1", "t2"], w=["t1"])
                else:
                    p.tt("dve", t3[:], B[pb][:], gt[gs][:, 2, :], ALU.mult, r=[bk(pb), ("gt", gs)], w=["t3"])
                    p.tt("pool", mixT[:, oc, :], t1[:], t3[:], ALU.add, r=["t1", "t3"], w=[("mix", oc)])
            yield
        for oc in range(8):
            ws = cn["wout"] % 2
            cn["wout"] += 1
            p.dma("sp", wout_sb[ws][:], wout[oc], w=[("wout", ws)])
            pb = 6 + cn["p1"] % 2
            cn["p1"] += 1
            for k in range(8):
                p.mm(B[pb][:], wout_sb[ws][:, k, :], mixT[:, k, :], start=(k == 0), stop=(k == 7),
                     r=[("wout", ws), ("mix", k)], w=[bk(pb)])
            xs = cn["xin"] % 2
            cn["xin"] += 1
            p.dma("sp", xin[xs][:], xT[:, oc, cols], w=[("xin", xs)])
            if t < 8:
                p.stt("dve", x1T[slot][:, oc, :], B[pb][:], ms[:, 16 + oc, 0:1], xin[xs][:], ALU.mult, ALU.add,
                      r=[bk(pb), "ms", ("xin", xs)], w=[("x1", slot)])
            else:
                for hf, r_ in ((0, 0), (1, 1)):
                    hs = slice(hf * 256, hf * 256 + 256)
                    p.stt("dve", x1T[slot][:, oc, hs], B[pb][:, hs], ms[:, 16 + oc, r_:r_ + 1], xin[xs][:, hs], ALU.mult, ALU.add,
                          r=[bk(pb), "ms", ("xin", xs)], w=[("x1", slot)])
            yield
        pb = 6 + cn["p1"] % 2
        cn["p1"] += 1
        for k in range(8):
            p.actf(sq[k % 2][:], x1T[slot][:, k, :], AF.Square, r=[("x1", slot)], w=[("sq", k % 2)])
            p.mm(B[pb][:], ones[:], sq[k % 2][:], start=(k == 0), stop=(k == 7), r=["ones", ("sq", k % 2)], w=[bk(pb)])
        p.actf(rstd[:], B[pb][:], AF.Sqrt, r=[bk(pb), "epsb"], w=["rstd"], scale=1.0 / 1024.0, bias=epsb[:])
        yield
        p.add("dve", lambda e: e.reciprocal(rstd[:], rstd[:]), r=["rstd"], w=["rstd"])
        for k in range(8):
            xk = xn[k % 2]
            xkt = ("xn", k % 2)
            p.tt("dve", xk[:], x1T[slot][:, k, :], rstd[:], ALU.mult, r=[("x1", slot), "rstd"], w=[xkt])
            halves = [(slice(0, 512), 0)] if t < 8 else [(slice(0, 256), 0), (slice(256, 512), 1)]
            for (hs, r_) in halves:
                if k % 2 == 0:
                    p.actf(h2T[slot][:, k, hs], xk[:, hs], AF.Identity, r=[xkt, "A2", "ms"], w=[("h2", slot)],
                           scale=A2[:, k, r_:r_ + 1], bias=ms[:, 24 + k, r_:r_ + 1])
                else:
                    p.ts("dve", h2T[slot][:, k, hs], xk[:, hs], A2[:, k, r_:r_ + 1], ms[:, 24 + k, r_:r_ + 1], ALU.mult, ALU.add,
                         r=[xkt, "A2", "ms"], w=[("h2", slot)])
        if t == 0:
            p.ts("dve", h2T[slot][:, :, 0:128], h2T[slot][:, :, 0:128], vm[:, 0:1], None, ALU.mult, r=["vm", ("h2", slot)], w=[("h2", slot)])
        if t == 8:
            p.ts("dve", h2T[slot][:, :, 128:256], h2T[slot][:, :, 128:256], vm[:, 1:2], None, ALU.mult, r=["vm", ("h2", slot)], w=[("h2", slot)])

    def hseg(c_lo, c_hi):
        out = []
        c = c_lo
        while c < c_hi:
            t = c // 512
            e = min(c_hi, (t + 1) * 512)
            out.append((t % NX1, slice(c - t * 512, e - t * 512), e - c))
            c = e
        return out

    def P2(u, ctx=False, g1=None):
        if not ctx:
            o0 = 128 + 512 * u
            NO = 512
            a0, NA = o0 - 64, 640
            r_ = 0
        else:
            o0, NO, a0, NA, r_ = 4352, 256, 4352, 256, 1
        apieces = hseg(a0, a0 + NA)
        opieces = hseg(o0, o0 + NO)
        for fc in range(22):
            ws = cn["wup"] % 4
            cn["wup"] += 1
            p.dma("sp", wup_sb[ws][:], wup[fc], w=[("wup", ws)])
            asl = cn["a"] % 2
            cn["a"] += 1
            off = 0
            for (sl, lsl, n) in apieces:
                pb = cn["ps"] % 6
                cn["ps"] += 1
                for k in range(8):
                    p.mm(B[pb][:, 0:n], wup_sb[ws][:, k, :], h2T[sl][:, k, lsl], start=(k == 0), stop=(k == 7),
                         r=[("wup", ws), ("h2", sl)], w=[bk(pb)])
                p.cp("act", a_sb[asl][:, off:off + n], B[pb][:, 0:n], r=[bk(pb)], w=[("a", asl)])
                off += n
            ws2 = cn["wup"] % 4
            cn["wup"] += 1
            p.dma("sp", wup_sb[ws2][:], wup[22 + fc], w=[("wup", ws2)])
            pv = cn["ps"] % 6
            cn["ps"] += 1
            off = 0
            for (sl, lsl, n) in opieces:
                for k in range(8):
                    p.mm(B[pv][:, off:off + n], wup_sb[ws2][:, k, :], h2T[sl][:, k, lsl], start=(k == 0), stop=(k == 7),
                         r=[("wup", ws2), ("h2", sl)], w=[bk(pv)])
                off += n
            ac = acc[asl]
            at = ("acc", asl)
            A = a_sb[asl]
            w = lambda i, j: fc_sb[:, fc, i * 3 + j:i * 3 + j + 1]
            if not ctx:
                a3 = A[:, 0:640].rearrange("p (r c) -> p r c", c=64)
                o3 = ac[:, :].rearrange("p (r c) -> p r c", c=64)
                p.ts("dve", ac[:], A[:, 64:576], w(1, 1), fc_sb[:, fc, 9:10], ALU.mult, ALU.add, r=[("a", asl), "fc"], w=[at])
                for i in range(3):
                    for j in range(3):
                        if i == 1 and j == 1:
                            continue
                        if j == 1:
                            src, dst = a3[:, i:i + 8, :], o3
                        elif j == 0:
                            src, dst = a3[:, i:i + 8, 0:63], o3[:, :, 1:64]
                        else:
                            src, dst = a3[:, i:i + 8, 1:64], o3[:, :, 0:63]
                        p.stt("dve", dst, src, w(i, j), dst, ALU.mult, ALU.add, r=[("a", asl), "fc", at], w=[at])
            else:
                p.ts("dve", ac[:, 0:256], A[:, 0:256], w(1, 1), fc_sb[:, fc, 9:10], ALU.mult, ALU.add, r=[("a", asl), "fc"], w=[at])
                p.stt("dve", ac[:, 1:256], A[:, 0:255], w(1, 0), ac[:, 1:256], ALU.mult, ALU.add, r=[("a", asl), "fc", at], w=[at])
                p.stt("dve", ac[:, 0:255], A[:, 1:256], w(1, 2), ac[:, 0:255], ALU.mult, ALU.add, r=[("a", asl), "fc", at], w=[at])
            p.actf(gl[asl][:, 0:NO], ac[:, 0:NO], AF.Gelu, r=[at], w=[("gl", asl)])
            p.tt("dve", uT[:, fc, 0:NO], gl[asl][:, 0:NO], B[pv][:, 0:NO], ALU.mult, r=[("gl", asl), bk(pv)], w=[("uT", fc)])
            if g1 is not None:
                next(g1, None)
        if g1 is not None:
            for _ in g1:
                pass
        for oc in range(8):
            ws = cn["wdn"] % 2
            cn["wdn"] += 1
            p.dma("sp", wdn_sb[ws][:], wdn[oc], w=[("wdn", ws)])
            pb = cn["ps"] % 6
            cn["ps"] += 1
            for fc in range(22):
                p.mm(B[pb][:, 0:NO], wdn_sb[ws][:, fc, :], uT[:, fc, 0:NO], start=(fc == 0), stop=(fc == 21),
                     r=[("wdn", ws), ("uT", fc)], w=[bk(pb)])
            off = 0
            if last:
                for (sl, lsl, n) in opieces:
                    p.stt("dve", xo[:, oc, off:off + n], B[pb][:, off:off + n], ms[:, 40 + oc, r_:r_ + 1], x1T[sl][:, oc, lsl], ALU.mult, ALU.add,
                          r=[bk(pb), "ms", ("x1", sl)], w=["xo"])
                    off += n
            else:
                xs_ = cn["xin"] % 2
                cn["xin"] += 1
                for (sl, lsl, n) in opieces:
                    p.stt("dve", xor[xs_][:, off:off + n], B[pb][:, off:off + n], ms[:, 40 + oc, r_:r_ + 1], x1T[sl][:, oc, lsl], ALU.mult, ALU.add,
                          r=[bk(pb), "ms", ("x1", sl)], w=[("xor", xs_)])
                    off += n
                oo_ = 512 * u if not ctx else 4096
                p.dma("pool", o_x[:, oc, oo_:oo_ + NO], xor[xs_][:, 0:NO], r=[("xor", xs_)], is_out=True)
        if last:
            pbn = 6 + cn["p1"] % 2
            cn["p1"] += 1
            for k in range(8):
                p.actf(sq[k % 2][:, 0:NO], xo[:, k, 0:NO], AF.Square, r=["xo"], w=[("sq", k % 2)])
                p.mm(B[pbn][:, 0:NO], ones[:], sq[k % 2][:, 0:NO], start=(k == 0), stop=(k == 7), r=["ones", ("sq", k % 2)], w=[bk(pbn)])
            p.actf(rstd2[:, 0:NO], B[pbn][:, 0:NO], AF.Sqrt, r=[bk(pbn), "epsb"], w=["rstd2"], scale=1.0 / 1024.0, bias=epsb[:])
            p.add("dve", lambda e: e.reciprocal(rstd2[:, 0:NO], rstd2[:, 0:NO]), r=["rstd2"], w=["rstd2"])
            p.tt("dve", xo[:, :, 0:NO], xo[:, :, 0:NO], rstd2[:, 0:NO].unsqueeze(1).broadcast_to([128, 8, NO]), ALU.mult,
                 r=["xo", "rstd2"], w=["xo"])
            for k in range(8):
                p.ts("dve", xo[:, k, 0:NO], xo[:, k, 0:NO], fg_sb[:, k:k + 1], None, ALU.mult, r=["xo", "fg"], w=["xo"])
        if last:
            oo = 512 * u
            p.dma("pool", o_x[:, :, oo:oo + NO], xo[:, :, 0:NO], r=["xo"], is_out=True)

    for _ in P1(0):
        pass
    for _ in P1(1):
        pass
    for t in range(1, 8):
        P2(t - 1, g1=P1(t + 1))
    P2(7)
    if not last:
        P2(0, ctx=True)
    return p.finish()

BF = ml_dtypes.bfloat16
RET_F = [math.log1p(-2.0 ** (-5.0 - h)) for h in range(4)]
RET_B = [math.log1p(-2.0 ** (-5.5 - h)) for h in range(4)]
W_SHAPES = [("w_in", (1024, 7712)), ("w_branch", (3, 512, 1024)), ("w_out", (1024, 1024)), ("w_up", (1024, 5632)), ("w_down", (2816, 1024))]


def _to_pk(v, nchunk):
    return np.ascontiguousarray(np.moveaxis(v.reshape(v.shape[:-1] + (nchunk, 128)), -1, 0))


def _feat_major(xe):
    T, D = xe.shape
    return np.ascontiguousarray(xe.T.reshape(D // 128, 128, T).transpose(1, 0, 2))


def _ext_tokens(xl, xc, s):
    D = xl.shape[1]
    out = np.zeros((TE, D), xl.dtype)
    lo = s * 4096 - 128
    a, b_ = max(lo, 0), min(lo + 4352, 16384)
    out[a - lo:b_ - lo] = xl[a:b_]
    out[4352:] = xc
    return out


def _ext_fm(full_lat, full_ctx, s):
    C = full_lat.shape[0]
    out = np.zeros((C, TE), full_lat.dtype)
    lo = s * 4096 - 128
    a, b_ = max(lo, 0), min(lo + 4352, 16384)
    out[:, a - lo:b_ - lo] = full_lat[:, a:b_]
    out[:, 4352:] = full_ctx
    return out


def _k2_consts():
    u = np.arange(128)[:, None]
    t = np.arange(128)[None, :]
    tri = np.zeros((128, 4, 128), np.float32)
    tri[:, 0] = np.where(u <= t, -1 / 16, 0)
    tri[:, 1] = np.where(u >= t, -1 / 16, 0)
    tri[:, 2] = np.where(u > t, -1 / 16, 0)
    tri[:, 3] = np.where(u < t, -1 / 16, 0)
    msk = np.zeros((128, 2, 128), np.float32)
    msk[:, 0] = (u <= t)
    msk[:, 1] = (u > t)
    return tri, msk


def _pos_feats(L):
    t = np.linspace(0.0, 1.0, L, dtype=np.float32)[:, None]
    ang = (np.float32(2.0 * math.pi / L) * np.arange(L, dtype=np.float32)[:, None]) * np.linspace(1e-4, 16 - 1.0, 16, dtype=np.float32)[None, :]
    ang = ang.astype(np.float64)
    return np.concatenate([t.astype(np.float64), np.cos(ang), -np.sin(ang)], -1)


def _window(L, chans):
    t = np.linspace(0.0, 1.0, L, dtype=np.float32)[:, None].astype(np.float64)
    mn = math.log(1e-2) / 1.5
    mx = math.log(1e-2) / 0.3
    deltas = np.abs(np.linspace(mn, mx, 512, dtype=np.float32)).astype(np.float64)[chans]
    return np.exp(-t * deltas[None, :]) + 0.05


def _hy_tables(L, chans):
    z = _pos_feats(L)
    w = _window(L, chans)
    zf = np.stack([z.T, z[::-1].T], 0).astype(np.float32)
    wn = np.stack([w.T, w[::-1].T], 0).astype(np.float32)
    return np.ascontiguousarray(zf), np.ascontiguousarray(wn)


def _lay_w(w):
    K, M = w.shape
    return np.ascontiguousarray(w.reshape(K // 128, 128, M // 128, 128).transpose(2, 1, 0, 3))


_NC_CACHE = {}


def _get_nc(name):
    if name not in _NC_CACHE:
        _NC_CACHE[name] = {"k0": build_k0, "k1": build_k1, "k2": build_k2, "k3": lambda: build_k3(False), "k3l": lambda: build_k3(True)}[name]()
    return _NC_CACHE[name]


def _run(name, ims):
    nc = _get_nc(name)
    res = run_bass_kernel_spmd(nc, ims, core_ids=list(range(8)))
    return res.results


def kernel(**inp):
    inp = {k: np.asarray(v) for k, v in inp.items()}
    f32 = np.float32
    flat = np.concatenate([np.ascontiguousarray(inp[n][l], dtype=f32).ravel() for l in range(2) for n, _ in W_SHAPES])
    NPC = flat.size // 8
    C3 = np.concatenate([inp["c"].astype(f32), inp["c_ctx"].astype(f32)[None]], 0)
    cT = np.ascontiguousarray(_to_pk(C3, 8).transpose(0, 2, 1))
    ims = []
    for i in range(8):
        adaw = np.ascontiguousarray(inp["ada_w"][:, :, i * 768:(i + 1) * 768], dtype=f32)
        adab = np.ascontiguousarray(inp["ada_b"][:, i * 768:(i + 1) * 768].reshape(2, 6, 128).transpose(2, 0, 1), dtype=f32)
        ims.append({"wsl": flat[i * NPC:(i + 1) * NPC].reshape(128, NWP), "adaw": adaw, "cT": cT, "adab": adab})
    r0 = _run("k0", ims)
    wflat = np.concatenate([np.asarray(r0[i]["wbf"]).reshape(-1) for i in range(8)])
    Wb = []
    off = 0
    for l in range(2):
        d = {}
        for n, shp in W_SHAPES:
            sz = int(np.prod(shp))
            d[n] = wflat[off:off + sz].reshape(shp)
            off += sz
        Wb.append(d)
    mods = []
    for l in range(2):
        m = np.zeros((3, 6144), f32)
        for i in range(8):
            mt = np.asarray(r0[i]["modT"])
            m[:, i * 768:(i + 1) * 768] = mt[:, l].transpose(2, 1, 0).reshape(3, 768)
        mods.append(m)

    tri, msk = _k2_consts()
    tabs = [(_hy_tables(LL, np.arange(64 * j, 64 * j + 64)), _hy_tables(256, np.arange(64 * j, 64 * j + 64))) for j in range(8)]

    x_lat = inp["x"].astype(f32)
    x_ctx = inp["ctx"].astype(f32)
    out = None
    for l in range(2):
        last = (l == 1)
        mod = mods[l]
        hs = np.concatenate([inp["hy_short_w"][l], inp["hy_short_b"][l][None]], 0).astype(f32)
        hsw = np.ascontiguousarray(_to_pk(hs, 12).transpose(0, 2, 1))
        xTs, modsels, vmlrs = [], [], []
        ims = []
        for i in range(8):
            b, s = i // 4, i % 4
            xTs.append(_feat_major(_ext_tokens(x_lat[b], x_ctx[b], s)))
            msl = np.stack([mod[b], mod[2]], 0)
            modsels.append(np.ascontiguousarray(_to_pk(msl, 48).transpose(0, 2, 1)).astype(f32))
            vmlrs.append(np.tile(np.array([[0.0 if s == 0 else 1.0, 0.0 if s == 3 else 1.0]], f32), (128, 1)))
            ims.append({"xT": xTs[i], "win": Wb[l]["w_in"], "modsel": modsels[i], "n1g": _to_pk(inp["norm1_g"][l].astype(f32), 8),
                        "hsw": hsw, "vmlr": vmlrs[i]})
        r1 = [{k: np.asarray(v) for k, v in r.items()} for r in _run("k1", ims)]
        def full_fm(key, b):
            lat = np.concatenate([r1[b * 4 + s][key][:, :, 128:4224] for s in range(4)], -1)
            ctxp = r1[b * 4][key][:, :, 4352:4608]
            return lat.reshape(-1, 16384), ctxp.reshape(-1, 256)
        qk_f = [full_fm("qk", b) for b in range(2)]
        z_f = [full_fm("z", b) for b in range(2)]
        kv_f = []
        lr_f = []
        for b in range(2):
            kv_f.append((np.concatenate([r1[b * 4 + s]["kv"][128:4224] for s in range(4)], 0), r1[b * 4]["kv"][4352:4608]))
            lr_f.append((np.concatenate([r1[b * 4 + s]["lrT"][:, 128:4224] for s in range(4)], -1), r1[b * 4]["lrT"][:, 4352:4608]))
        ims = []
        for j in range(8):
            b, h = j // 4, j % 4
            d = {}
            qkl, qkc = qk_f[b]
            qkall = np.concatenate([qkc, qkl], -1)
            rows = lambda base: slice(base + h * 64, base + h * 64 + 64)
            d["qT"] = np.ascontiguousarray(np.stack([qkall[rows(0)], qkall[rows(512)]], 0))
            d["kT"] = np.ascontiguousarray(np.stack([qkall[rows(256)], qkall[rows(768)]], 0))
            kvl, kvc = kv_f[b]
            kvall = np.concatenate([kvc, kvl], 0)
            d["ktok"] = np.ascontiguousarray(np.stack([kvall[:, h * 64:h * 64 + 64], kvall[:, 768 + h * 64:768 + h * 64 + 64]], 0))
            d["vtok"] = np.ascontiguousarray(np.stack([kvall[:, 256 + h * 128:256 + h * 128 + 128], kvall[:, 1024 + h * 128:1024 + h * 128 + 128]], 0))
            lrl, lrc = lr_f[b]
            lrall = np.concatenate([lrc, lrl], -1)
            lr34 = np.ones((34, TT), f32)
            lr34[0:16] = lrall[0:16]
            lr34[17:33] = lrall[16:32]
            d["lrT"] = lr34
            wa = np.zeros((34, 128), f32)
            wa2 = inp["gla_wa2"][l].astype(f32)
            ba = inp["gla_ba"][l].astype(f32)
            wa[0:16, :64] = wa2[0][:, h * 64:h * 64 + 64]
            wa[16, :64] = ba[0][h * 64:h * 64 + 64]
            wa[17:33, 64:] = wa2[1][:, h * 64:h * 64 + 64]
            wa[33, 64:] = ba[1][h * 64:h * 64 + 64]
            d["wa"] = wa
            laR = np.zeros((128, 2, 64), f32)
            laR[:, 0, :] = -16 * RET_F[h]
            laR[:, 1, :] = -16 * RET_B[h]
            d["laR"] = laR
            d["tri"] = tri
            d["msk"] = msk
            (zfL, wnL), (zfC, wnC) = tabs[j]
            d["zf"], d["win"], d["zfc"], d["winc"] = zfL, wnL, zfC, wnC
            d["hw1"] = np.ascontiguousarray(inp["hy_w1"][l], dtype=f32)
            d["hb"] = np.ascontiguousarray(np.stack([inp["hy_b1"][l], inp["hy_b2"][l][0], inp["hy_b2"][l][1], inp["hy_freq"][l]], 1), dtype=f32)
            d["hw2"] = np.ascontiguousarray(inp["hy_w2"][l].transpose(1, 0, 2), dtype=f32)
            w3 = inp["hy_w3"][l].astype(f32)
            d["hw3"] = np.ascontiguousarray(np.stack([w3[:, 64 * j:64 * j + 64], w3[:, 512 + 64 * j:512 + 64 * j + 64]], 1))
            ch = slice(64 * j, 64 * j + 64)
            zl = np.stack([z_f[bb][0][ch] for bb in range(2)], 0)
            zc = np.stack([z_f[bb][1][ch] for bb in range(2)], 0)
            def lay(zz, nb, rev):
                a = zz.reshape(2, 64, nb, 128)
                if rev:
                    a = a[:, :, :, ::-1]
                return np.ascontiguousarray(a.transpose(3, 1, 2, 0)).reshape(128, 64, 2 * nb)
            d["zrev"], d["znat"] = lay(zl, 128, True), lay(zl, 128, False)
            d["zrevc"], d["znatc"] = lay(zc, 2, True), lay(zc, 2, False)
            d["hbias"] = np.ascontiguousarray(np.tile(inp["hy_bias"][l][ch][None].astype(f32), (128, 1)))
            ims.append(d)
        r2 = [{k: np.asarray(v) for k, v in r.items()} for r in _run("k2", ims)]
        y_lat = np.zeros((2, 512, 16384), BF)
        y_ctx = np.zeros((2, 512, 256), BF)
        for j in range(8):
            Y = r2[j]["Y"].reshape(64, 128, 128, 2)
            y_lat[:, 64 * j:64 * j + 64] = Y.transpose(3, 0, 2, 1).reshape(2, 64, 16384)
            Yc = r2[j]["Yc"].reshape(64, 128, 2, 2)
            y_ctx[:, 64 * j:64 * j + 64] = Yc.transpose(3, 0, 2, 1).reshape(2, 64, 256)
        wbr_l = np.stack([_lay_w(Wb[l]["w_branch"][g]) for g in range(3)], 0)
        wout_l = _lay_w(Wb[l]["w_out"])
        wup_l = _lay_w(Wb[l]["w_up"])
        wdn_l = _lay_w(Wb[l]["w_down"])
        fconv = np.concatenate([inp["ffn_conv_w"][l].reshape(9, 2816), inp["ffn_conv_b"][l][None]], 0).astype(f32)
        fconv = np.ascontiguousarray(_to_pk(fconv, 22).transpose(0, 2, 1))
        ims = []
        for i in range(8):
            b, s = i // 4, i % 4
            d = {"xT": xTs[i], "sg": r1[i]["sg"], "x0": r1[i]["x0"], "gate": r1[i]["gate"]}
            for key, g in (("onr", 0), ("ong", 1)):
                on_b = np.stack([r2[b * 4 + h]["onT"][g] for h in range(4)], 0)
                d[key] = _ext_fm(on_b[:, :, 256:].reshape(512, 16384), on_b[:, :, :256].reshape(512, 256), s).reshape(4, 128, TE)
            d["yh"] = _ext_fm(y_lat[b], y_ctx[b], s).reshape(4, 128, TE)
            d["wbr"], d["wout"], d["wup"], d["wdn"] = wbr_l, wout_l, wup_l, wdn_l
            d["modsel"] = modsels[i]
            d["n2g"] = _to_pk(inp["norm2_g"][l].astype(f32), 8)
            d["fconv"] = fconv
            d["fing"] = _to_pk(inp["final_g"].astype(f32), 8)
            d["vmlr"] = vmlrs[i]
            ims.append(d)
        r3 = _run("k3l" if last else "k3", ims)
        xo = [np.asarray(r["xo"]) for r in r3]
        new_lat = np.zeros((2, 16384, 1024), f32)
        for i in range(8):
            b, s = i // 4, i % 4
            new_lat[b, s * 4096:(s + 1) * 4096] = xo[i][:, :, :4096].transpose(2, 1, 0).reshape(4096, 1024)
        if not last:
            x_ctx = np.stack([xo[b * 4][:, :, 4096:].transpose(2, 1, 0).reshape(256, 1024) for b in range(2)], 0)
        x_lat = new_lat
    return x_lat
```
